# Optimizing a Trainium2 kernel written in Bass

```python
import math
import jax, jax.numpy as jnp
from jax import lax
import numpy as np

D_MODEL = 1024
BATCH = 4
SEQ = 8192
DEPTH = 1

CHUNK = 64
Q_BLOCK = 128
MEM_LEN = 256
MIX_WIDTH = D_MODEL
FOX_WIDTH = MIX_WIDTH // 2
FOX_HEAD_DIM = 64
FOX_HEADS = FOX_WIDTH // FOX_HEAD_DIM
GLA_WIDTH = MIX_WIDTH - FOX_WIDTH
GLA_HEADS = 4
GLA_VALUE_DIM = GLA_WIDTH // GLA_HEADS
GLA_KEY_DIM = GLA_VALUE_DIM // 2
GLA_KEY_WIDTH = GLA_HEADS * GLA_KEY_DIM
GLA_GATE_RANK = 16
GLA_TAU = 16.0
XATTN_HEADS = 4
XATTN_HEAD_DIM = D_MODEL // XATTN_HEADS
D_FF = 4 * D_MODEL
EPS = 1e-6
IN_SIZES = (FOX_WIDTH, FOX_WIDTH, FOX_WIDTH, FOX_HEADS,
            GLA_KEY_WIDTH, GLA_KEY_WIDTH, GLA_WIDTH, GLA_GATE_RANK, GLA_WIDTH)
IN_WIDTH = sum(IN_SIZES)

kernel_name = "hybrid_fox_gla_memxattn_block"


def _split_points():
    return [int(v) for v in np.cumsum(np.array(IN_SIZES))[:-1]]


def rms_norm(x, g):
    xf = x.astype(jnp.float32)
    y = xf * lax.rsqrt(jnp.mean(xf * xf, axis=-1, keepdims=True) + EPS)
    return (y * g.astype(jnp.float32)).astype(x.dtype)


def fox_attention(q, k, v, log_f):
    B, S, H, D = q.shape
    scale = 1.0 / math.sqrt(D)
    c = jnp.transpose(jnp.cumsum(log_f, axis=1), (0, 2, 1))
    outs = []
    for i in range(S // Q_BLOCK):
        start, end = i * Q_BLOCK, (i + 1) * Q_BLOCK
        qb = q[:, start:end]
        kb = k[:, :end]
        vb = v[:, :end]
        s = jnp.einsum('bqhd,bkhd->bhqk', qb, kb).astype(jnp.float32) * scale
        bias = c[:, :, start:end, None] - c[:, :, None, :end]
        q_pos = start + jnp.arange(Q_BLOCK)
        k_pos = jnp.arange(end)
        mask = k_pos[None, :] <= q_pos[:, None]
        s = jnp.where(mask, s + bias, -jnp.inf)
        p = jax.nn.softmax(s, axis=-1)
        outs.append(jnp.einsum('bhqk,bkhd->bqhd', p.astype(vb.dtype), vb))
    return jnp.concatenate(outs, axis=1)


def gla_chunked(q, k, v, g_log):
    B, S, H, Dk = q.shape
    Dv = v.shape[-1]
    N = S // CHUNK

    def to_chunks(a):
        return jnp.transpose(a.astype(jnp.float32).reshape(B, N, CHUNK, H, a.shape[-1]), (0, 3, 1, 2, 4))

    qc = to_chunks(q) * (Dk ** -0.5)
    kc = to_chunks(k)
    vc = to_chunks(v)
    bc = jnp.cumsum(to_chunks(g_log), axis=3)
    q_dec = qc * jnp.exp(bc)
    k_dec = kc * jnp.exp(-bc)
    A = jnp.einsum('bhnck,bhnsk->bhncs', q_dec, k_dec)
    tril = jnp.tril(jnp.ones((CHUNK, CHUNK), dtype=bool))
    A = jnp.where(tril, A, 0.0)
    o_intra = jnp.einsum('bhncs,bhnsv->bhncv', A, vc)
    b_last = bc[:, :, :, -1:, :]
    dS = jnp.einsum('bhnck,bhncv->bhnkv', kc * jnp.exp(b_last - bc), vc)
    decay = jnp.exp(b_last[:, :, :, 0, :])

    def step(state, inp):
        d, ds = inp
        return d[..., None] * state + ds, state

    init = jnp.zeros((B, H, Dk, Dv), jnp.float32)
    _, s_prev = lax.scan(step, init, (jnp.moveaxis(decay, 2, 0), jnp.moveaxis(dS, 2, 0)))
    s_prev = jnp.moveaxis(s_prev, 0, 2)
    o_inter = jnp.einsum('bhnck,bhnkv->bhncv', q_dec, s_prev)
    o = o_intra + o_inter
    o = jnp.transpose(o, (0, 2, 3, 1, 4)).reshape(B, S, H, Dv)
    return o.astype(v.dtype)


def parallel_mixer(xn, w_in, fox_b_f, fox_q_norm_g, fox_k_norm_g,
                   gla_w_gate2, gla_b_gate, gla_out_norm_g, w_out):
    B, S, _ = xn.shape
    proj = xn @ w_in
    fq, fk, fv, ff, gq, gk, gv, glr, gr = jnp.split(proj, _split_points(), axis=-1)
    fq = rms_norm(fq.reshape(B, S, FOX_HEADS, FOX_HEAD_DIM), fox_q_norm_g)
    fk = rms_norm(fk.reshape(B, S, FOX_HEADS, FOX_HEAD_DIM), fox_k_norm_g)
    fv = fv.reshape(B, S, FOX_HEADS, FOX_HEAD_DIM)
    log_f = jax.nn.log_sigmoid((ff + fox_b_f).astype(jnp.float32))
    fox_out = fox_attention(fq, fk, fv, log_f).reshape(B, S, FOX_WIDTH)
    g_log = jax.nn.log_sigmoid((glr @ gla_w_gate2 + gla_b_gate).astype(jnp.float32)) / GLA_TAU
    g_log = g_log.reshape(B, S, GLA_HEADS, GLA_KEY_DIM)
    gla_o = gla_chunked(gq.reshape(B, S, GLA_HEADS, GLA_KEY_DIM),
                        gk.reshape(B, S, GLA_HEADS, GLA_KEY_DIM),
                        gv.reshape(B, S, GLA_HEADS, GLA_VALUE_DIM), g_log)
    gla_o = rms_norm(gla_o, gla_out_norm_g.reshape(GLA_HEADS, GLA_VALUE_DIM)).reshape(B, S, GLA_WIDTH)
    gla_out = gla_o * jax.nn.silu(gr)
    return jnp.concatenate([fox_out, gla_out], axis=-1) @ w_out


def memory_cross_attention(hn, memn, wq, wkv, q_norm_g, k_norm_g, wo):
    B, S, _ = hn.shape
    M = memn.shape[1]
    q = rms_norm((hn @ wq).reshape(B, S, XATTN_HEADS, XATTN_HEAD_DIM), q_norm_g)
    k, v = jnp.split(memn @ wkv, 2, axis=-1)
    k = rms_norm(k.reshape(B, M, XATTN_HEADS, XATTN_HEAD_DIM), k_norm_g)
    v = v.reshape(B, M, XATTN_HEADS, XATTN_HEAD_DIM)
    s = jnp.einsum('bshd,bmhd->bhsm', q, k).astype(jnp.float32) / math.sqrt(XATTN_HEAD_DIM)
    p = jax.nn.softmax(s, axis=-1)
    o = jnp.einsum('bhsm,bmhd->bshd', p.astype(v.dtype), v).reshape(B, S, D_MODEL)
    return o @ wo


def setup_inputs(seed: int = 0) -> dict:
    key = jax.random.key(seed)
    ks = jax.random.split(key, 24)
    f32 = jnp.float32

    def nrm(k, shape, scale):
        return jax.random.normal(k, shape, f32) * scale

    def gain(k, shape):
        return jnp.ones(shape, f32) + 0.02 * jax.random.normal(k, shape, f32)

    return {
        "x": nrm(ks[0], (BATCH, SEQ, D_MODEL), 1.0),
        "mem": nrm(ks[1], (BATCH, MEM_LEN, D_MODEL), 1.0),
        "norm_mix_g": gain(ks[2], (D_MODEL,)),
        "w_in": nrm(ks[3], (D_MODEL, IN_WIDTH), D_MODEL ** -0.5),
        "fox_b_f": 2.0 + 0.1 * jax.random.normal(ks[4], (FOX_HEADS,), f32),
        "fox_q_norm_g": gain(ks[5], (FOX_HEAD_DIM,)),
        "fox_k_norm_g": gain(ks[6], (FOX_HEAD_DIM,)),
        "gla_w_gate2": nrm(ks[7], (GLA_GATE_RANK, GLA_KEY_WIDTH), GLA_GATE_RANK ** -0.5),
        "gla_b_gate": nrm(ks[8], (GLA_KEY_WIDTH,), 0.02),
        "gla_out_norm_g": gain(ks[9], (GLA_WIDTH,)),
        "w_out": nrm(ks[10], (MIX_WIDTH, D_MODEL), MIX_WIDTH ** -0.5),
        "norm_xattn_g": gain(ks[11], (D_MODEL,)),
        "norm_mem_g": gain(ks[12], (D_MODEL,)),
        "xattn_wq": nrm(ks[13], (D_MODEL, D_MODEL), D_MODEL ** -0.5),
        "xattn_wkv": nrm(ks[14], (D_MODEL, 2 * D_MODEL), D_MODEL ** -0.5),
        "xattn_q_norm_g": gain(ks[15], (XATTN_HEAD_DIM,)),
        "xattn_k_norm_g": gain(ks[16], (XATTN_HEAD_DIM,)),
        "xattn_wo": nrm(ks[17], (D_MODEL, D_MODEL), D_MODEL ** -0.5),
        "norm_mlp_g": gain(ks[18], (D_MODEL,)),
        "mlp_w1": nrm(ks[19], (D_MODEL, D_FF), D_MODEL ** -0.5),
        "mlp_w2": nrm(ks[20], (D_FF, D_MODEL), D_FF ** -0.5),
    }


def reference(x, mem, norm_mix_g, w_in, fox_b_f, fox_q_norm_g, fox_k_norm_g,
              gla_w_gate2, gla_b_gate, gla_out_norm_g, w_out,
              norm_xattn_g, norm_mem_g, xattn_wq, xattn_wkv, xattn_q_norm_g,
              xattn_k_norm_g, xattn_wo, norm_mlp_g, mlp_w1, mlp_w2):
    h = x
    for _ in range(DEPTH):
        h = h + parallel_mixer(rms_norm(h, norm_mix_g), w_in, fox_b_f, fox_q_norm_g,
                               fox_k_norm_g, gla_w_gate2, gla_b_gate, gla_out_norm_g, w_out)
        h = h + memory_cross_attention(rms_norm(h, norm_xattn_g), rms_norm(mem, norm_mem_g),
                                       xattn_wq, xattn_wkv, xattn_q_norm_g, xattn_k_norm_g, xattn_wo)
        u = jax.nn.relu(rms_norm(h, norm_mlp_g) @ mlp_w1)
        h = h + (u * u) @ mlp_w2
    return h
```

```python
import math
from contextlib import ExitStack
import numpy as np
import concourse.bass as bass
import concourse.mybir as mybir
from concourse.bass_utils import run_bass_kernel_spmd
from concourse.alu_op_type import AluOpType as ALU

F32 = mybir.dt.float32
BF16 = mybir.dt.bfloat16
AF = mybir.ActivationFunctionType

D = 1024
EPS = 1e-6
NEG = -30000.0
ENGS = ["pe", "act", "dve", "pool", "sp"]
SAME_ENGINE_SYNC = True
POOL_DMA_DEPTH = 4
STOP = 9

W_OUT, W_Q, W_KV, W_O, W_1, W_2 = 0, 2, 4, 8, 10, 18
NWT = 26
G_MIX, G_X, G_MEM, G_MLP, G_XQ, G_XK, G_FQ, G_FK = 0, 8, 16, 24, 32, 34, 36, 37


class Prog:
    def __init__(self, nc, es):
        self.nc = nc
        self.es = es
        self.ops = []
        self.last_w = {}
        self.readers = {}
        self.sems = {}
        self.emitted = 0
        self.cnt = {e: 0 for e in ENGS}
        self.dma_cnt = {}
        self.waited = {e: {} for e in ENGS}
        self.pending_barrier = {e: set() for e in ENGS}
        self.dma_since_barrier = []
        self.out_keys = {}
        self.enabled = True
        self.pool_dmas = []
        self._cap = None
        self._sink = None

    def sem(self, key):
        if key not in self.sems:
            name = "s_" + "_".join(str(k) for k in key)
            self.sems[key] = self.es.enter_context(self.nc.semaphore(name))
        return self.sems[key]

    def start_capture(self):
        self._cap = []

    def end_capture(self):
        c, self._cap = self._cap, None
        return c

    def replay(self, items):
        for it in items:
            self.add(*it)

    def sink_start(self):
        self._sink = []

    def sink_end(self):
        c, self._sink = self._sink, None
        return c

    @staticmethod
    def merge(a, b):
        out, i, j = [], 0, 0
        while i < len(a) or j < len(b):
            if j >= len(b) or (i < len(a) and i * len(b) <= j * len(a)):
                out.append(a[i]); i += 1
            else:
                out.append(b[j]); j += 1
        return out

    def add(self, eng, fn, r=(), w=(), dma=None):
        if not self.enabled:
            return None
        if self._cap is not None:
            self._cap.append((eng, fn, tuple(r), tuple(w), dma))
            return None
        if self._sink is not None:
            self._sink.append((eng, fn, tuple(r), tuple(w), dma))
            return None
        thr = None
        if eng == "pool" and dma is not None:
            n = len(self.pool_dmas)
            dma = ("pq", n % POOL_DMA_DEPTH)
            if n >= POOL_DMA_DEPTH:
                thr = self.pool_dmas[n - POOL_DMA_DEPTH]
            self.pool_dmas.append(len(self.ops))
        i = len(self.ops)
        deps = set()
        for res in r:
            if res in self.last_w:
                deps.add(self.last_w[res])
        for res in w:
            if res in self.last_w:
                deps.add(self.last_w[res])
            deps.update(self.readers.get(res, ()))
        for res in r:
            self.readers.setdefault(res, []).append(i)
        for res in w:
            self.last_w[res] = i
            self.readers[res] = []
        deps.discard(i)
        if self.pending_barrier[eng]:
            deps.update(self.pending_barrier[eng])
            self.pending_barrier[eng] = set()
        if thr is not None:
            deps.add(thr)
        self.ops.append(dict(eng=eng, fn=fn, deps=deps, dma=dma, semkey=None, val=0))
        if dma is not None:
            self.dma_since_barrier.append(i)
        return i

    def barrier(self):
        last = {}
        for i in range(len(self.ops) - 1, -1, -1):
            op = self.ops[i]
            if op["dma"] is None and op["eng"] not in last:
                last[op["eng"]] = i
            if len(last) == len(ENGS):
                break
        s = set(last.values()) | set(self.dma_since_barrier)
        self.dma_since_barrier = []
        for e in ENGS:
            self.pending_barrier[e] = set(s)

    def _skip(self, prod, cons):
        if prod["dma"] is None and cons["dma"] is None and prod["eng"] == cons["eng"]:
            if prod["eng"] == "pe":
                return True
            if not SAME_ENGINE_SYNC:
                return True
        return False

    def emit(self):
        nc = self.nc
        lo, hi = self.emitted, len(self.ops)
        needed = set()
        for i in range(lo, hi):
            op = self.ops[i]
            for d in op["deps"]:
                if d < lo or self._skip(self.ops[d], op):
                    continue
                needed.add(d)
        for i in range(lo, hi):
            op = self.ops[i]
            if op["dma"] is not None:
                k = op["dma"]
                self.dma_cnt[k] = self.dma_cnt.get(k, 0) + 1
                op["semkey"] = ("d",) + tuple(k)
                op["val"] = 16 * self.dma_cnt[k]
            elif i in needed or op.get("force"):
                self.cnt[op["eng"]] += 1
                op["semkey"] = ("e", op["eng"])
                op["val"] = self.cnt[op["eng"]]
            self_sem = op["semkey"]
            if self_sem is not None:
                self.sem(self_sem)
        handles = {}
        with nc.Block() as block:
            def mk(eng_name):
                def body(h):
                    for i in range(lo, hi):
                        op = self.ops[i]
                        if op["eng"] != eng_name:
                            continue
                        reqs = {}
                        for d in op["deps"]:
                            p = self.ops[d]
                            if self._skip(p, op):
                                continue
                            if p["semkey"] is None:
                                assert d < lo
                                continue
                            reqs[p["semkey"]] = max(reqs.get(p["semkey"], 0), p["val"])
                        wd = self.waited[eng_name]
                        for key, val in reqs.items():
                            if wd.get(key, 0) >= val:
                                continue
                            h.wait_ge(self.sems[key], val)
                            wd[key] = val
                        ins = op["fn"](h)
                        if ins is None:
                            continue
                        if op["semkey"] is not None:
                            ins.then_inc(self.sems[op["semkey"]], 16 if op["dma"] is not None else 1)
                return body
            block.tensor(mk("pe"))
            block.scalar(mk("act"))
            block.vector(mk("dve"))
            block.gpsimd(mk("pool"))
            block.sync(mk("sp"))
        self.emitted = hi

    def force_last(self):
        seen = set()
        for i in range(len(self.ops) - 1, self.emitted - 1, -1):
            op = self.ops[i]
            if op["dma"] is None and op["eng"] not in seen:
                op["force"] = True
                seen.add(op["eng"])


def build(S):
    NB = S // 128
    NJ = NB // 2
    NTA = S // 512
    NTO = S // 1024
    NCOL = NB * 8
    NG = (NCOL + 127) // 128
    GW = min(128, NCOL)
    NCOLO = NJ * 8
    NGO = (NCOLO + 127) // 128
    GWO = min(128, NCOLO)

    nc = bass.Bass("TRN2", target_bir_lowering=False)

    def din(name, shape, dt=F32):
        return nc.dram_tensor(name, list(shape), dt, kind="ExternalInput").ap()

    xT_all = din("xT_all", [NTA, 128, 8, 512])
    xT_own = din("xT_own", [NTO, 128, 8, 512])
    memT = din("memT", [128, 8, 256])
    w_ff = din("w_ff", [128, 8, 8])
    w_fox = din("w_fox", [4, 128, 8, 384])
    w_gla = din("w_gla", [128, 8, 1552])
    w_gate2a = din("w_gate2a", [33, 256])
    w_all = din("w_all", [NWT, 128, 4096])
    gcols_d = din("gcols", [128, 38])
    bf_bc_d = din("bf_bc", [128, NCOL])
    gout_bc_d = din("gout_bc", [128, 512])
    cst_d = din("cst", [128, 5, 128])
    mask_fox_d = din("mask_fox", [128, 512])
    gla_mask_d = din("gla_mask", [128, 512])
    parf_d = din("parf", [128, 2])
    yT = nc.dram_tensor("yT", [NTO, 128, 8, 512], F32, kind="ExternalOutput").ap()
    xn_all = nc.dram_tensor("xn_all", [NTA, 128, 8, 512], BF16, kind="Internal").ap()
    xn_own = nc.dram_tensor("xn_own", [NTO, 128, 8, 512], BF16, kind="Internal").ap()
    wsc = nc.dram_tensor("wsc", [NWT, 128, 4096], BF16, kind="Internal").ap()

    with ExitStack() as es:
        P = Prog(nc, es)

        def sb(name, shape, dt, stack=es):
            return stack.enter_context(nc.sbuf_tensor("sb_" + name, list(shape), dt))

        def mm(out, lhsT, rhs, start=True, stop=True, r=(), w=()):
            P.add("pe", lambda e: e.matmul(out, lhsT, rhs, start=start, stop=stop, skip_group_check=True), r, w)

        def tr(out, in_, ident, r=(), w=()):
            P.add("pe", lambda e: e.transpose(out, in_, ident), r, w)

        def act(out, in_, func, r=(), w=(), scale=1.0, bias=0.0):
            P.add("act", lambda e: e.activation(out=out, in_=in_, func=func, bias=bias, scale=scale), r, w)

        def tt(eng, out, in0, in1, op, r=(), w=()):
            P.add(eng, lambda e: e.tensor_tensor(out=out, in0=in0, in1=in1, op=op), r, w)

        def ts(eng, out, in0, s1, op0, r=(), w=(), s2=None, op1=None):
            if op1 is None:
                P.add(eng, lambda e: e.tensor_scalar(out=out, in0=in0, scalar1=s1, scalar2=None, op0=op0), r, w)
            else:
                P.add(eng, lambda e: e.tensor_scalar(out=out, in0=in0, scalar1=s1, scalar2=s2, op0=op0, op1=op1), r, w)

        def stt(out, in0, scalar, in1, op0, op1, r=(), w=()):
            P.add("dve", lambda e: e.scalar_tensor_tensor(out=out, in0=in0, scalar=scalar, in1=in1, op0=op0, op1=op1), r, w)

        def cp(eng, out, in_, r=(), w=()):
            if eng == "act":
                P.add("act", lambda e: e.activation(out=out, in_=in_, func=AF.Copy), r, w)
            else:
                P.add(eng, lambda e: e.tensor_copy(out=out, in_=in_), r, w)

        def memset(eng, ap, val, w=()):
            P.add(eng, lambda e: e.memset(ap, val), (), w)

        def dma(eng, out, in_, key, r=(), w=()):
            P.add(eng, lambda e: e.dma_start(out=out, in_=in_), r, w, dma=key)

        ps = es.enter_context(nc.psum_tensor("ps", [128, 8, 512], F32))
        pbf = ps[:, 7, :].bitcast(BF16)
        PS = lambda b: ("ps", b)

        mixG = sb("mixG", [128, 4, NJ * 128], BF16)
        gcols = sb("gcols", [128, 38], F32)
        cstf = sb("cstf", [128, 5, 128], F32)
        cstb = sb("cstb", [128, 5, 128], BF16)
        zerob = sb("zerob", [128, 256], BF16)
        parf = sb("parf", [128, 2], F32)
        maskb = sb("maskb", [128, 512], BF16)
        glamask = sb("glamask", [128, 512], BF16)
        goutbc = sb("goutbc", [128, 512], F32)
        Ck = [sb("Ck%d" % s, [128, NG, 128], BF16) for s in range(3)]
        Cq = [sb("Cq%d" % s, [128, NGO, 128], BF16) for s in range(3)]
        identf, tri_incl, tri_rev, onesf = (cstf[:, i, :] for i in range(4))
        identb, onesb, bdonesb = cstb[:, 0, :], cstb[:, 3, :], cstb[:, 4, :]

        dma("sp", gcols[:], gcols_d[:, :], ("c", 0), w=["gcols"])
        dma("sp", cstf[:], cst_d[:, :, :], ("c", 1), w=["cstf"])
        dma("pool", cstb[:], cst_d[:, :, :], ("c", 2), w=["cstb"])
        dma("sp", parf[:], parf_d[:, :], ("c", 3), w=["parf"])
        dma("pool", maskb[:], mask_fox_d[:, :], ("c", 4), w=["maskb"])
        dma("pool", glamask[:], gla_mask_d[:, :], ("c", 5), w=["glamask"])
        dma("sp", goutbc[:], gout_bc_d[:, :], ("c", 6), w=["goutbc"])
        memset("dve", zerob[:], 0.0, w=["zerob"])

        with ExitStack() as ph:
            P.enabled = STOP >= 1
            xin = [sb("xin%d" % i, [128, 8, 512], F32, ph) for i in range(2)]
            xn = [sb("xn%d" % i, [128, 8, 512], BF16, ph) for i in range(2)]
            sq = [sb("sq%d" % i, [128, 512], BF16, ph) for i in range(3)]
            lnv = sb("lnv", [128, 512], F32, ph)
            rstd = sb("rstd", [128, 512], F32, ph)
            wff = sb("wff", [128, 8, 8], BF16, ph)
            xob = [sb("xob%d" % i, [128, 8, 256], BF16, ph) for i in range(2)]
            tmpb = [sb("tmpb%d" % i, [128, 8, 128], BF16, ph) for i in range(2)]
            bfbc = sb("bfbc", [128, NCOL], F32, ph)
            L = sb("L", [128, NCOL], F32, ph)
            Lp = [sb("Lp%d" % i, [128, NJ, 8], F32, ph) for i in range(2)]
            Pex = sb("Pex", [128, NCOL], F32, ph)
            Lown = sb("Lown", [128, NCOLO], F32, ph)
            Pown = sb("Pown", [128, NCOLO], F32, ph)
            tmpo = sb("tmpo", [128, NCOLO], F32, ph)
            CT = sb("CT", [128, NG, 128], F32, ph)
            CTo = sb("CTo", [128, NGO, 128], F32, ph)
            R1 = sb("R1", [128, NG, 128], F32, ph)

            dma("pool", wff[:], w_ff[:, :, :], ("n", 0), w=["wff"])
            dma("sp", bfbc[:], bf_bc_d[:, :], ("n", 1), w=["bfbc"])
            wg = sb("wg", [128, 8, 1552], BF16, ph)
            g2a = sb("g2a", [33, 256], BF16, ph)
            glra = sb("glra", [33, 128], BF16, ph)
            glro = sb("glro", [33, 128], BF16, ph)
            ez = sb("ez", [128, 256], F32, ph)
            Gt = sb("Gt", [128, 256], F32, ph)
            E2 = sb("E2", [128, 256], F32, ph)
            kd2 = sb("kd2", [128, 256], BF16, ph)
            vb = sb("vb", [128, 512], BF16, ph)
            dec = sb("dec", [128, 4], F32, ph)
            St = [sb("St%d" % p, [128, 128], F32, ph) for p in range(2)]
            Sm = [sb("Sm%d" % p, [128, 128], F32, ph) for p in range(2)]
            Sd = [sb("Sd%d" % p, [128, 128], F32, ph) for p in range(2)]
            Sbf = [sb("Sbf%d" % p, [128, 128], BF16, ph) for p in range(2)]
            ezo = sb("ezo", [128, 256], F32, ph)
            Go = sb("Go", [128, 256], F32, ph)
            Eq = sb("Eq", [128, 256], F32, ph)
            Ek = sb("Ek", [128, 256], F32, ph)
            qdTm = [[sb("qdTm%d%d" % (p, hh), [128, 128], BF16, ph) for hh in range(2)] for p in range(2)]
            kdT = sb("kdT", [128, 2, 128], BF16, ph)
            vbo = sb("vbo", [128, 512], BF16, ph)
            ATm = sb("ATm", [128, 512], BF16, ph)
            sqo = sb("sqo", [128, 512], F32, ph)
            ss4 = sb("ss4", [128, 4], F32, ph)
            ln4 = sb("ln4", [128, 4], F32, ph)
            rs4 = sb("rs4", [128, 4], F32, ph)
            eg = sb("eg", [128, 512], F32, ph)
            sg = sb("sg", [128, 512], F32, ph)
            glo = sb("glo", [128, 512], BF16, ph)

            dma("pool", wg[:], w_gla[:, :, :], ("wg", 0), w=["wg"])
            memset("dve", g2a[:], 0.0, w=["g2a"])
            dma("pool", g2a[:], w_gate2a[:, :], ("wg", 1), w=["g2a"])
            for t_, nm in ((glra, "glra"), (glro, "glro")):
                memset("dve", t_[:], 0.0, w=[nm])
                memset("dve", t_[32:33, :], 1.0, w=[nm])
            for p in range(2):
                memset("dve", St[p][:], 0.0, w=[("St", p)])
                for hh in range(2):
                    memset("dve", qdTm[p][hh][:], 0.0, w=["qdT"])

            def gates_a(src, src_res, blk, glr_t, glr_res, gb=0):
                for k in range(8):
                    mm(ps[0:16, gb, 0:128], wg[:, k, 1536:1552], src[:, k, blk * 128:(blk + 1) * 128],
                       start=(k == 0), stop=(k == 7), r=["wg", src_res], w=[PS(gb)])
                cp("act", glr_t[0:16, :], ps[0:16, gb, 0:128], r=[PS(gb)], w=[glr_res])

            def gates_b(glr_t, glr_res, ez_t, G_t, G_res, gb=0):
                mm(ps[:, gb, 256:512], glr_t[0:33, :], g2a[0:33, :], r=[glr_res, "g2a"], w=[PS(gb)])
                act(ez_t[:], ps[:, gb, 256:512], AF.Exp, r=[PS(gb)], w=[G_res + "e"], scale=-1.0)
                act(G_t[:], ez_t[:], AF.Ln, r=[G_res + "e"], w=[G_res], bias=1.0)

            sqi = 0

            def load_tile(n):
                dma("sp", xin[n % 2][:], xT_all[n], ("xin", n % 2), w=[("xin", n % 2)])

            def n_part(T):
                nonlocal sqi
                i = T % 2
                if T + 1 < NTA:
                    load_tile(T + 1)
                for k in range(8):
                    s_ = sqi % 3
                    sqi += 1
                    act(sq[s_][:], xin[i][:, k, :], AF.Square, r=[("xin", i)], w=[("sq", s_)])
                    mm(ps[:, 6, :], onesb, sq[s_][:], start=(k == 0), stop=(k == 7),
                       r=[("sq", s_), "cstb"], w=[PS(6)])
                act(lnv[:], ps[:, 6, :], AF.Ln, r=[PS(6)], w=["lnv"], scale=1.0 / D, bias=EPS)
                act(rstd[:], lnv[:], AF.Exp, r=["lnv"], w=["rstd"], scale=-0.5)
                for k in range(8):
                    stt(xn[i][:, k, :], xin[i][:, k, :], gcols[:, G_MIX + k:G_MIX + k + 1], rstd[:],
                        ALU.mult, ALU.mult, r=[("xin", i), "rstd", "gcols"], w=[("xn", i)])
                dma("pool", xn_all[T], xn[i][:], ("xnst", i), r=[("xn", i)], w=[("xnd", "all", T)])
                for jl2 in range(2):
                    E_ = xn[i][:, :, jl2 * 256:jl2 * 256 + 128]
                    O_ = xn[i][:, :, jl2 * 256 + 128:jl2 * 256 + 256]
                    ts("dve", tmpb[jl2][:], E_, parf[:, 1:2], ALU.mult, r=[("xn", i), "parf"], w=[("tmpb", jl2)])
                    stt(xob[i][:, :, jl2 * 128:(jl2 + 1) * 128], O_, parf[:, 0:1], tmpb[jl2][:], ALU.mult, ALU.add,
                        r=[("xn", i), ("tmpb", jl2), "parf"], w=[("xob", i)])
                dma("pool", xn_own[T // 2][:, :, (T % 2) * 256:(T % 2 + 1) * 256], xob[i][:], ("xobst", i),
                    r=[("xob", i)], w=[("xnd", "own", T // 2, T % 2)])
                per = (NWT + NTA - 1) // NTA
                for wi in range(T * per, min(NWT, (T + 1) * per)):
                    dma("pool", wsc[wi], w_all[wi], ("wc", wi % 4), w=[("wsc", wi)])
                for b in range(4):
                    for k in range(8):
                        mm(ps[:, 6, b * 8:(b + 1) * 8], xn[i][:, k, b * 128:(b + 1) * 128], wff[:, k, :],
                           start=(k == 0), stop=(k == 7), r=[("xn", i), "wff"], w=[PS(6)])
                tt("dve", L[:, T * 32:(T + 1) * 32], ps[:, 6, 0:32], bfbc[:, T * 32:(T + 1) * 32], ALU.add,
                   r=[PS(6), "bfbc"], w=["L"])

            held_h1, held_h2 = [], []
            def g_tile(T):
                nonlocal held_h1, held_h2
                for b in range(4):
                    g = T * 4 + b
                    e_par = g % 2
                    j = g // 2
                    blk = slice(b * 128, (b + 1) * 128)
                    if P.enabled:
                        P.start_capture()
                    gates_a(xn[T % 2], ("xn", T % 2), b, glra, "glra")
                    for k in range(8):
                        mm(ps[:, 1, 0:256], xn[T % 2][:, k, blk], wg[:, k, 256:512], start=(k == 0), stop=(k == 7),
                           r=["wg", ("xn", T % 2)], w=[PS(1)])
                    gates_b(glra, "glra", ez, Gt, "Gt")
                    for k in range(8):
                        mm(ps[:, 2, :], xn[T % 2][:, k, blk], wg[:, k, 512:1024], start=(k == 0), stop=(k == 7),
                           r=["wg", ("xn", T % 2)], w=[PS(2)])
                    cp("act", vb[:], ps[:, 2, :], r=[PS(2)], w=["vb"])
                    mm(ps[:, 3, 0:256], tri_rev, Gt[:], r=["Gt", "cstf"], w=[PS(3)])
                    for p in range(2):
                        mm(ps[:, 3, 256 + 2 * p:258 + 2 * p], Gt[:, p * 128:(p + 1) * 128], onesf[:, 0:2],
                           start=False, stop=True, r=["Gt", "cstf"], w=[PS(3)])
                    act(E2[:], ps[:, 3, 0:256], AF.Exp, r=[PS(3)], w=["E2"], scale=-1.0 / 16)
                    act(dec[:], ps[:, 3, 256:260], AF.Exp, r=[PS(3)], w=["dec"], scale=-1.0 / 16)
                    tt("dve", kd2[:], ps[:, 1, 0:256], E2[:], ALU.mult, r=[PS(1), "E2"], w=["kd2"])
                    for h in range(4):
                        mm(ps[(h % 2) * 64:(h % 2) * 64 + 64, 1, 256 + (h // 2) * 128:256 + (h // 2 + 1) * 128],
                           kd2[:, h * 64:(h + 1) * 64], vb[:, h * 128:(h + 1) * 128],
                           r=["kd2", "vb"], w=[PS(1)])
                    for p in range(2):
                        dsv = ps[:, 1, 256 + p * 128:256 + (p + 1) * 128]
                        if e_par == 0:
                            stt(Sm[p][:], St[p][:], dec[:, 2 * p:2 * p + 1], dsv, ALU.mult, ALU.add,
                                r=[("St", p), "dec", PS(1)], w=[("Sm", p)])
                        else:
                            stt(St[p][:], Sm[p][:], dec[:, 2 * p:2 * p + 1], dsv, ALU.mult, ALU.add,
                                r=[("Sm", p), "dec", PS(1)], w=[("St", p)])
                    if e_par != 0:
                        if P.enabled:
                            capB = P.end_capture()
                            P.replay(Prog.merge(held_h1, capB))
                        continue
                    if P.enabled:
                        capB = P.end_capture()
                        P.replay(Prog.merge(held_h2, capB))
                        held_h2 = []
                    for p in range(2):
                        tt("dve", Sd[p][:], Sm[p][:], St[p][:], ALU.subtract, r=[("Sm", p), ("St", p)], w=[("Sd", p)])
                        stt(Sbf[p][:], Sd[p][:], parf[:, 0:1], St[p][:], ALU.mult, ALU.add,
                            r=[("Sd", p), ("St", p), "parf"], w=[("Sbf", p)])
                    if P.enabled:
                        P.start_capture()
                    jl = j % 2
                    oblk = slice(jl * 128, (jl + 1) * 128)
                    xres = ("xob", T % 2)
                    gates_a(xob[T % 2], xres, jl, glro, "glro", gb=7)
                    for k in range(8):
                        mm(ps[:, 5, :], xob[T % 2][:, k, oblk], wg[:, k, 1024:1536], start=(k == 0), stop=(k == 7),
                           r=["wg", xres], w=[PS(5)])
                    gates_b(glro, "glro", ezo, Go, "Go", gb=7)
                    act(eg[:], ps[:, 5, :], AF.Exp, r=[PS(5)], w=["eg"], scale=-1.0)
                    act(eg[:], eg[:], AF.Ln, r=["eg"], w=["eg"], bias=1.0)
                    act(eg[:], eg[:], AF.Exp, r=["eg"], w=["eg"], scale=-1.0)
                    tt("dve", sg[:], ps[:, 5, :], eg[:], ALU.mult, r=[PS(5), "eg"], w=["sg"])
                    tt("pool", sg[:], sg[:], goutbc[:], ALU.mult, r=["sg", "goutbc"], w=["sg"])
                    for qi in range(4):
                        c0 = qi * 128
                        for k in range(8):
                            mm(ps[:, 4, qi * 128:(qi + 1) * 128], wg[:, k, c0:c0 + 128], xob[T % 2][:, k, oblk],
                               start=(k == 0), stop=(k == 7), r=["wg", xres], w=[PS(4)])
                    for k in range(8):
                        mm(ps[:, 5, :], xob[T % 2][:, k, oblk], wg[:, k, 512:1024], start=(k == 0), stop=(k == 7),
                           r=["wg", xres], w=[PS(5)])
                    cp("act", vbo[:], ps[:, 5, :], r=[PS(5)], w=["vbo"])
                    for p in range(2):
                        mm(ps[:, 7, p * 128:(p + 1) * 128], Go[:, p * 128:(p + 1) * 128], tri_incl,
                           r=["Go", "cstf"], w=[PS(7)])
                    act(Eq[:], ps[:, 7, 0:256], AF.Exp, r=[PS(7)], w=["Eq"], scale=-1.0 / 16)
                    act(Ek[:], ps[:, 7, 0:256], AF.Exp, r=[PS(7)], w=["Ek"], scale=1.0 / 16)
                    for p in range(2):
                        for hh in range(2):
                            rows = slice(hh * 64, hh * 64 + 64)
                            stt(qdTm[p][hh][rows, :], ps[rows, 4, p * 128:(p + 1) * 128], 0.125,
                                Eq[rows, p * 128:(p + 1) * 128], ALU.mult, ALU.mult, r=[PS(4), "Eq"], w=["qdT"])
                    tt("dve", kdT[:].rearrange("p a b -> p (a b)"), ps[:, 4, 256:512], Ek[:], ALU.mult,
                       r=[PS(4), "Ek"], w=["kdT"])
                    if P.enabled:
                        held_h1 = P.end_capture()
                        P.start_capture()
                    for h in range(4):
                        pr = slice((h % 2) * 64, (h % 2) * 64 + 64)
                        mm(ps[:, 5, h * 128:(h + 1) * 128], kdT[:, h // 2, :], qdTm[h // 2][h % 2][:],
                           r=["kdT", "qdT"], w=[PS(5)])
                    tt("dve", ATm[:], ps[:, 5, :], glamask[:], ALU.mult, r=[PS(5), "glamask"], w=["ATm"])
                    for h in range(4):
                        pr = slice((h % 2) * 64, (h % 2) * 64 + 64)
                        mm(ps[:, 4, h * 128:(h + 1) * 128], ATm[:, h * 128:(h + 1) * 128], vbo[:, h * 128:(h + 1) * 128],
                           start=True, stop=False, r=["ATm", "vbo"], w=[PS(4)])
                        mm(ps[:, 4, h * 128:(h + 1) * 128], qdTm[h // 2][h % 2][:], Sbf[h // 2][:],
                           start=False, stop=True, r=["qdT", ("Sbf", h // 2)], w=[PS(4)])
                    act(sqo[:], ps[:, 4, :], AF.Square, r=[PS(4)], w=["sqo"])
                    P.add("dve", lambda e_, o_=ss4[:], i_=sqo[:].rearrange("p (h d) -> p h d", h=4):
                          e_.tensor_reduce(out=o_, in_=i_, axis=mybir.AxisListType.X, op=ALU.add), ["sqo"], ["ss4"])
                    act(ln4[:], ss4[:], AF.Ln, r=["ss4"], w=["ln4"], scale=1.0 / 128, bias=EPS)
                    act(rs4[:], ln4[:], AF.Exp, r=["ln4"], w=["rs4"], scale=-0.5)
                    for h in range(4):
                        stt(glo[:, h * 128:(h + 1) * 128], ps[:, 4, h * 128:(h + 1) * 128], rs4[:, h:h + 1],
                            sg[:, h * 128:(h + 1) * 128], ALU.mult, ALU.mult, r=[PS(4), "rs4", "sg"], w=["glo"])
                    for h in range(4):
                        tr(pbf[:, h * 128:(h + 1) * 128], glo[:, h * 128:(h + 1) * 128], identb, r=["glo", "cstb"], w=[PS(7)])
                    cp("act", mixG[:, 0:4, j * 128:(j + 1) * 128], pbf[:, 0:512].rearrange("p (c t) -> p c t", c=4),
                       r=[PS(7)], w=[("mix", 4 + c, j) for c in range(4)])
                    if P.enabled:
                        held_h2 = P.end_capture()

            load_tile(0)
            n_part(0)
            for T in range(NTA):
                capN = []
                if T + 1 < NTA and P.enabled:
                    P.start_capture()
                    n_part(T + 1)
                    capN = P.end_capture()
                if P.enabled:
                    P.sink_start()
                g_tile(T)
                if P.enabled:
                    tile_ops = P.sink_end()
                    cut = int(0.55 * len(tile_ops))
                    P.replay(Prog.merge(tile_ops[:cut], capN) + tile_ops[cut:])
            if P.enabled:
                P.replay(held_h2)
            act(L[:], L[:], AF.Exp, r=["L"], w=["L"], scale=-1.0)
            act(L[:], L[:], AF.Ln, r=["L"], w=["L"], bias=1.0)
            L4 = L[:].rearrange("p (j e h) -> p j e h", e=2, h=8)
            Pex4 = Pex[:].rearrange("p (j e h) -> p j e h", e=2, h=8)
            tt("dve", Lp[0][:], L4[:, :, 0, :], L4[:, :, 1, :], ALU.add, r=["L"], w=[("Lp", 0)])
            cur = 0
            s_ = 1
            while s_ < NJ:
                nxt = 1 - cur
                tt("dve", Lp[nxt][:, s_:, :], Lp[cur][:, s_:, :], Lp[cur][:, :NJ - s_, :], ALU.add,
                   r=[("Lp", cur)], w=[("Lp", nxt)])
                cp("dve", Lp[nxt][:, :s_, :], Lp[cur][:, :s_, :], r=[("Lp", cur)], w=[("Lp", nxt)])
                cur = nxt
                s_ *= 2
            tt("dve", Pex4[:, :, 0, :], Lp[cur][:], L4[:, :, 0, :], ALU.subtract, r=[("Lp", cur), "L"], w=["Pex"])
            tt("dve", Pex4[:, :, 0, :], Pex4[:, :, 0, :], L4[:, :, 1, :], ALU.subtract, r=["Pex", "L"], w=["Pex"])
            tt("dve", Pex4[:, :, 1, :], Pex4[:, :, 0, :], L4[:, :, 0, :], ALU.add, r=["Pex", "L"], w=["Pex"])
            Lo3 = Lown[:].rearrange("p (j h) -> p j h", h=8)
            Po3 = Pown[:].rearrange("p (j h) -> p j h", h=8)
            To3 = tmpo[:].rearrange("p (j h) -> p j h", h=8)
            ts("dve", To3, L4[:, :, 1, :], parf[:, 0:1], ALU.mult, r=["L", "parf"], w=["tmpo"])
            stt(Lo3, L4[:, :, 0, :], parf[:, 1:2], To3, ALU.mult, ALU.add, r=["L", "tmpo", "parf"], w=["Lown"])
            ts("dve", To3, Pex4[:, :, 1, :], parf[:, 0:1], ALU.mult, r=["Pex", "parf", "Lown"], w=["tmpo"])
            stt(Po3, Pex4[:, :, 0, :], parf[:, 1:2], To3, ALU.mult, ALU.add, r=["Pex", "tmpo", "parf"], w=["Pown"])
            for G in range(NG):
                mm(ps[0:GW, 2, G * 128:(G + 1) * 128], L[:, G * GW:(G + 1) * GW], tri_incl, start=True, stop=False,
                   r=["L", "cstf"], w=[PS(2)])
                mm(ps[0:GW, 2, G * 128:(G + 1) * 128], Pex[:, G * GW:(G + 1) * GW], onesf, start=False, stop=True,
                   r=["Pex", "cstf"], w=[PS(2)])
            for G in range(NGO):
                mm(ps[0:GWO, 3, G * 128:(G + 1) * 128], Lown[:, G * GWO:(G + 1) * GWO], tri_incl, start=True, stop=False,
                   r=["Lown", "cstf"], w=[PS(3)])
                mm(ps[0:GWO, 3, G * 128:(G + 1) * 128], Pown[:, G * GWO:(G + 1) * GWO], onesf, start=False, stop=True,
                   r=["Pown", "cstf"], w=[PS(3)])
            CTf = CT[:].rearrange("p g t -> p (g t)")
            CTof = CTo[:].rearrange("p g t -> p (g t)")
            R1f = R1[:].rearrange("p g t -> p (g t)")
            cp("act", CTf[0:GW, :], ps[0:GW, 2, 0:NG * 128], r=[PS(2)], w=["CT"])
            ts("dve", CTof[0:GWO, :], ps[0:GWO, 3, 0:NGO * 128], -1.0, ALU.mult, r=[PS(3)], w=["CTo"])
            for (src, dstl, gw, ng, nm) in ((CTf, Ck, GW, NG, "CT"), (CTof, Cq, GWO, NGO, "CTo")):
                d0 = dstl[0][:].rearrange("p g t -> p (g t)")
                d1 = dstl[1][:].rearrange("p g t -> p (g t)")
                d2 = dstl[2][:].rearrange("p g t -> p (g t)")
                n = ng * 128
                cp("dve", d0[0:gw, :], src[0:gw, :], r=[nm], w=[("C", nm, 0)])
                tt("dve", R1f[0:gw, 0:n], src[0:gw, :], d0[0:gw, :], ALU.subtract, r=[nm, ("C", nm, 0)], w=["R1"])
                cp("dve", d1[0:gw, :], R1f[0:gw, 0:n], r=["R1"], w=[("C", nm, 1)])
                tt("dve", R1f[0:gw, 0:n], R1f[0:gw, 0:n], d1[0:gw, :], ALU.subtract, r=["R1", ("C", nm, 1)], w=["R1"])
                cp("dve", d2[0:gw, :], R1f[0:gw, 0:n], r=["R1"], w=[("C", nm, 2)])
            P.force_last()
            P.barrier()
            P.emit()

        mixF = sb("mixF", [128, 4, NJ * 128], BF16)
        with ExitStack() as ph:
            P.enabled = STOP >= 2
            KT = [sb("KT%d" % h, [128, S], BF16, ph) for h in range(2)]
            QT = [sb("QT%d" % h, [128, NJ * 128], BF16, ph) for h in range(2)]
            Vaug = sb("Vaug", [128, NB, 2, 65], BF16, ph)
            wf = sb("wf", [128, 8, 384], BF16, ph)
            xa = [sb("xa%d" % i, [128, 8, 512], BF16, ph) for i in range(3)]
            xo = [sb("xo%d" % i, [128, 8, 512], BF16, ph) for i in range(2)]
            sqk = [sb("sqk%d" % i, [128, 512], BF16, ph) for i in range(3)]
            rsk = [sb("rsk%d" % i, [128, 512], F32, ph) for i in range(3)]
            PSETS = [(4, 5), (0, 1), (3, 2)]
            NPT = 6
            PT = [sb("PT%d" % i, [128, 512], BF16, ph) for i in range(NPT)]
            fo = [sb("fo%d" % i, [128, 128], BF16, ph) for i in range(2)]
            rdens = [sb("rdenf%d" % i, [128, 2, 1], F32, ph) for i in range(2)]
            junk = sb("junk", [128, 2], F32, ph)
            AUG0 = [64, 0]
            KROWS = [70, 128]
            SBK = [0, 1, 3, 5, 6]
            ACCB = 2

            memset("pool", Vaug[:, :, :, 64:65], 1.0, w=["Vones"])
            memset("pool", KT[1][0:64, :], 0.0, w=[("KTaugm", 1), ("KTaugR", 1)])
            memset("pool", QT[1][0:64, :], 0.0, w=[("QTaugm", 1), ("QTaugR", 1)])
            xa_i = 0
            xo_i = 0
            sq_i = 0
            pt_i = 0
            ss_i = 0
            fo_i = 0
            pending_tail = [None]
            for hp in range(4):
                dma("pool", wf[:], w_fox[hp], ("wf", 0), w=["wf"])
                for hh in range(2):
                    h = 2 * hp + hh
                    a0 = AUG0[hh]
                    if hp == 0:
                        memset("dve", KT[hh][a0:a0 + 6, :], 1.0, w=[("KTaugm", hh), ("KTaugR", hh)])
                        memset("dve", QT[hh][a0:a0 + 6, :], 1.0, w=[("QTaugm", hh), ("QTaugR", hh)])
                    else:
                        memset("pool", junk[:, 0:1], 0.0,
                               w=[("KTaugm", hh), ("KTaugR", hh), ("QTaugm", hh), ("QTaugR", hh), "junk"])
                    kres, qres = [], []
                    for s_ in range(3):
                        for G in range(NG):
                            nblk = GW // 8
                            dma("pool", KT[hh][a0 + 3 + s_:a0 + 4 + s_, G * nblk * 128:(G + 1) * nblk * 128],
                                Ck[s_][h:GW:8, G, :], ("aug", 0),
                                r=[("C", "CT", s_), ("KTaugm", hh)], w=[("KTaugd", hh, s_, G)])
                            kres.append(("KTaugd", hh, s_, G))
                        for G in range(NGO):
                            nblk = GWO // 8
                            dma("pool", QT[hh][a0 + s_:a0 + 1 + s_, G * nblk * 128:(G + 1) * nblk * 128],
                                Cq[s_][h:GWO:8, G, :], ("aug", 1),
                                r=[("C", "CTo", s_), ("QTaugm", hh)], w=[("QTaugd", hh, s_, G)])
                            qres.append(("QTaugd", hh, s_, G))
                    memset("pool", junk[:, 0:1], 0.0, w=[("KTaugR", hh), "junk"])
                    if P.enabled:
                        P.ops[-1]["deps"].update(P.last_w[r_] for r_ in kres)
                    memset("pool", junk[:, 1:2], 0.0, w=[("QTaugR", hh), "junk"])
                    if P.enabled:
                        P.ops[-1]["deps"].update(P.last_w[r_] for r_ in qres)

                def proj_norm2(src, src_res, col0, dsts, cols, dst_res_fn, gcol, lnbias):
                    nonlocal sq_i
                    s_ = sq_i % 3
                    sq_i += 1
                    pb, qb_ = PSETS[s_]
                    for k in range(8):
                        mm(ps[:, pb, :], wf[:, k, col0:col0 + 128], src[:, k, :],
                           start=(k == 0), stop=(k == 7), r=["wf", src_res], w=[PS(pb)])
                    act(sqk[s_][:], ps[:, pb, :], AF.Square, r=[PS(pb)], w=[("sqk", s_)])
                    mm(ps[:, qb_, :], bdonesb, sqk[s_][:], r=[("sqk", s_), "cstb"], w=[PS(qb_)])
                    act(rsk[s_][:], ps[:, qb_, :], AF.Ln, r=[PS(qb_)], w=[("rsk", s_)], scale=1.0 / 64, bias=EPS)
                    act(rsk[s_][:], rsk[s_][:], AF.Exp, r=[("rsk", s_)], w=[("rsk", s_)], scale=-0.5, bias=lnbias)
                    for hh in range(2):
                        rows = slice(hh * 64, hh * 64 + 64)
                        stt(dsts[hh][rows, cols], ps[rows, pb, :], gcols[rows, gcol:gcol + 1], rsk[s_][rows, :],
                            ALU.mult, ALU.mult, r=[PS(pb), ("rsk", s_), "gcols"], w=[dst_res_fn(hh)])

                def proj_tile(T):
                    nonlocal xa_i, xo_i
                    a = xa_i % 3
                    xa_i += 1
                    dma("sp", xa[a][:], xn_all[T], ("xa", a), r=[("xnd", "all", T)], w=[("xa", a)])
                    proj_norm2(xa[a], ("xa", a), 128, KT, slice(T * 512, (T + 1) * 512),
                               lambda hh, T=T: ("KT", hh, T), G_FK, 0.0)
                    for b in range(4):
                        for k in range(8):
                            mm(ps[:, 6, b * 128:(b + 1) * 128], xa[a][:, k, b * 128:(b + 1) * 128], wf[:, k, 256:384],
                               start=(k == 0), stop=(k == 7), r=["wf", ("xa", a)], w=[PS(6)])
                    cp("dve", Vaug[:, T * 4:(T + 1) * 4, :, 0:64],
                       ps[:, 6, :].rearrange("p (b h d) -> p b h d", b=4, h=2), r=[PS(6)], w=[("V", T)])
                    if T % 2 == 0:
                        t = T // 2
                        o = xo_i % 2
                        xo_i += 1
                        dma("sp", xo[o][:], xn_own[t], ("xo", o), r=[("xnd", "own", t, 0), ("xnd", "own", t, 1)], w=[("xo", o)])
                        proj_norm2(xo[o], ("xo", o), 0, QT, slice(t * 512, (t + 1) * 512),
                                   lambda hh, t=t: ("QT", hh, t), G_FQ, math.log(0.125))

                for T in range(NTA):
                    proj_tile(T)
                steps_l = [(j, kp) for j in range(NJ) for kp in range(j + 1)]

                def s_mm(idx):
                    nonlocal ss_i
                    j, kp = steps_l[idx]
                    t = j // 4
                    sbk = SBK[ss_i % 5]
                    ss_i += 1
                    diag = (kp == j)
                    if diag:
                        mm(ps[:, sbk, :], identb, maskb[:], start=True, stop=False, r=["cstb", "maskb"], w=[PS(sbk)])
                    for e in range(2):
                        for hh in range(2):
                            kb = 2 * kp + e
                            kr = KROWS[hh]
                            mm(ps[:, sbk, (e * 2 + hh) * 128:(e * 2 + hh + 1) * 128],
                               KT[hh][0:kr, kb * 128:(kb + 1) * 128], QT[hh][0:kr, j * 128:(j + 1) * 128],
                               start=(not diag), stop=True,
                               r=[("KT", hh, kb // 4), ("KTaugR", hh), ("QT", hh, t), ("QTaugR", hh)], w=[PS(sbk)])
                    return sbk

                q_ = [s_mm(i_) for i_ in range(min(4, len(steps_l)))]
                for idx, (j, kp) in enumerate(steps_l):
                    ab = (ACCB, 4)[j % 2]
                    sbk = q_.pop(0)
                    if idx + 4 < len(steps_l):
                        q_.append(s_mm(idx + 4))
                    if kp == 0:
                        if pending_tail[0] is not None:
                            pending_tail[0]()
                            pending_tail[0] = None
                        mm(ps[:, ab, 0:130], zerob[:, 0:128], zerob[:, 0:130], start=True, stop=False, r=["zerob"], w=[PS(ab)])
                    p_ = pt_i % NPT
                    pt_i += 1
                    act(PT[p_][:], ps[:, sbk, :], AF.Exp, r=[PS(sbk)], w=[("PT", p_)])
                    for e in range(2):
                        for hh in range(2):
                            kb = 2 * kp + e
                            mm(ps[:, ab, hh * 65:(hh + 1) * 65], PT[p_][:, (e * 2 + hh) * 128:(e * 2 + hh + 1) * 128],
                               Vaug[:, kb, hh, :], start=False, stop=(kp == j and e == 1),
                               r=[("PT", p_), ("V", kb // 4), "Vones"], w=[PS(ab)])
                    if kp != j:
                        continue
                    acc3 = ps[:, ab, 0:130].rearrange("p (h d) -> p h d", h=2)
                    rden = rdens[j % 2]
                    P.add("dve", lambda e_, o_=rden[:], i_=acc3[:, :, 64:65]: e_.reciprocal(out=o_, in_=i_),
                          [PS(ab)], [("rdenf", j % 2)])
                    f_ = fo_i % 2
                    fo_i += 1
                    for hh in range(2):
                        ts("dve", fo[f_][:, hh * 64:(hh + 1) * 64], ps[:, ab, hh * 65:hh * 65 + 64], rden[:, hh, :], ALU.mult,
                           r=[PS(ab), ("rdenf", j % 2)], w=[("fo", f_)])

                    def tail(f_=f_, hp=hp, j=j):
                        tr(pbf[:, 0:128], fo[f_][:], identb, r=[("fo", f_), "cstb"], w=[PS(7)])
                        cp("dve", mixF[:, hp, j * 128:(j + 1) * 128], pbf[:, 0:128], r=[PS(7)], w=[("mix", hp, j)])
                    pending_tail[0] = tail
            if pending_tail[0] is not None:
                pending_tail[0]()
                pending_tail[0] = None
            P.force_last()
            P.barrier()
            P.emit()

        with ExitStack() as ph:
            P.enabled = STOP >= 4
            NWB = 3
            wt = [sb("wt%d" % i, [128, 8, 512], BF16, ph) for i in range(NWB)]
            hT = sb("hT", [128, 8, 512], F32, ph)
            hT2 = sb("hT2", [128, 8, 512], F32, ph)
            uT2 = sb("uT2", [128, 8, 512], BF16, ph)
            hn = sb("hn", [128, 8, 512], BF16, ph)
            sqb = [sb("sqb%d" % i, [128, 512], BF16, ph) for i in range(4)]
            lnv = sb("plnv", [128, 512], F32, ph)
            rstd = sb("prstd", [128, 512], F32, ph)
            qraw = sb("qraw", [128, 8, 512], F32, ph)
            qn = sb("qn", [128, 8, 512], BF16, ph)
            PTx = [[sb("PTx%d%d" % (a_, i), [128, 512], BF16, ph) for i in range(2)] for a_ in range(2)]
            rdn = [sb("rdn%d" % i, [128, 512], F32, ph) for i in range(2)]
            r32 = [sb("r32%d" % i, [128, 512], F32, ph) for i in range(2)]
            knT = sb("knT", [128, 8, 256], BF16, ph)
            vm = sb("vm", [128, 2, 1024], BF16, ph)
            memn = sb("memn", [128, 8, 256], BF16, ph)

            wt_i = [0]

            def load_w(idx):
                b = wt_i[0] % NWB
                wt_i[0] += 1
                dma("sp", wt[b][:].rearrange("p k n -> p (k n)"), wsc[idx], ("wt", b), r=[("wsc", idx)], w=[("wt", b)])
                return b

            pb_i = [0]

            def pbank():
                b = pb_i[0] % 4
                pb_i[0] += 1
                return b

            sq_i = [0]

            def nsq():
                s_ = sq_i[0] % 4
                sq_i[0] += 1
                return s_

            def rms_rstd(srcs, n_feat, lnbias=0.0, npart=128, ncol=512):
                n = len(srcs)
                for i, (ap, res) in enumerate(srcs):
                    s_ = nsq()
                    act(sqb[s_][:, 0:ncol], ap, AF.Square, r=res, w=[("sqb", s_)])
                    mm(ps[:, 4, 0:ncol], onesb, sqb[s_][:, 0:ncol], start=(i == 0), stop=(i == n - 1),
                       r=[("sqb", s_), "cstb"], w=[PS(4)])
                act(lnv[:, 0:ncol], ps[:, 4, 0:ncol], AF.Ln, r=[PS(4)], w=["plnv"], scale=1.0 / n_feat, bias=EPS)
                act(rstd[:, 0:ncol], lnv[:, 0:ncol], AF.Exp, r=["plnv"], w=["prstd"], scale=-0.5, bias=lnbias)

            dma("sp", qraw[:, :, 0:256], memT[:, :, :], ("mem", 0), w=["qraw"])
            rms_rstd([(qraw[:, k, 0:256], ["qraw"]) for k in range(8)], D, ncol=256)
            for k in range(8):
                stt(memn[:, k, :], qraw[:, k, 0:256], gcols[:, G_MEM + k:G_MEM + k + 1], rstd[:, 0:256], ALU.mult, ALU.mult,
                    r=["qraw", "prstd", "gcols"], w=["memn"])
            for og in range(2):
                b = load_w(W_KV + og)
                for mc in range(4):
                    c = og * 4 + mc
                    pbk = pbank()
                    for k in range(8):
                        mm(ps[:, pbk, 0:256], wt[b][:, k, mc * 128:(mc + 1) * 128], memn[:, k, :], start=(k == 0), stop=(k == 7),
                           r=[("wt", b), "memn"], w=[PS(pbk)])
                    cp("act", hT[:, c, 0:256], ps[:, pbk, 0:256], r=[PS(pbk)], w=[("hT", 0, c)])
            for h in range(4):
                rms_rstd([(hT[:, 2 * h + dc, 0:256], [("hT", 0, 2 * h + dc)]) for dc in range(2)], 256, ncol=256)
                for dc in range(2):
                    c = 2 * h + dc
                    stt(knT[:, c, :], hT[:, c, 0:256], gcols[:, G_XK + dc:G_XK + dc + 1], rstd[:, 0:256], ALU.mult, ALU.mult,
                        r=[("hT", 0, c), "prstd", "gcols"], w=["knT"])
            for og in range(2):
                b = load_w(W_KV + 2 + og)
                for mb in range(2):
                    pbk = pbank()
                    for k in range(8):
                        mm(ps[:, pbk, :], memn[:, k, mb * 128:(mb + 1) * 128], wt[b][:, k, :], start=(k == 0), stop=(k == 7),
                           r=[("wt", b), "memn"], w=[PS(pbk)])
                    cp("act", vm[:, mb, og * 512:(og + 1) * 512], ps[:, pbk, :], r=[PS(pbk)], w=["vm"])

            hTs = [hT, hT2]
            uTs = [qn, uT2]
            ures = lambda fg, fc: ("qn", fc) if fg % 2 == 0 else ("uT2", fc)

            def finish_norm(gbase, H, bt):
                act(lnv[:], ps[:, 4, :], AF.Ln, r=[PS(4)], w=["plnv"], scale=1.0 / D, bias=EPS)
                act(rstd[:], lnv[:], AF.Exp, r=["plnv"], w=["prstd"], scale=-0.5)
                for k in range(8):
                    stt(hn[:, k, :], H[:, k, :], gcols[:, gbase + k:gbase + k + 1], rstd[:], ALU.mult, ALU.mult,
                        r=[("hT", bt, k), "prstd", "gcols"], w=[("hn", k)])

            def proj_add(widx, src_fn, src_res_fn, H, bt, sumsq):
                pend = []
                for og in range(2):
                    b_ = load_w(widx + og)
                    for mc in range(4):
                        m = og * 4 + mc
                        pbk = pbank()
                        for k in range(8):
                            mm(ps[:, pbk, :], wt[b_][:, k, mc * 128:(mc + 1) * 128], src_fn(k), start=(k == 0), stop=(k == 7),
                               r=[("wt", b_)] + src_res_fn(k), w=[PS(pbk)])
                        tt("dve", H[:, m, :], ps[:, pbk, :], H[:, m, :], ALU.add, r=[PS(pbk), ("hT", bt, m)], w=[("hT", bt, m)])
                        if sumsq:
                            s_ = nsq()
                            act(sqb[s_][:], H[:, m, :], AF.Square, r=[("hT", bt, m)], w=[("sqb", s_)])

                            def ssq(m=m, s_=s_):
                                mm(ps[:, 4, :], onesb, sqb[s_][:], start=(m == 0), stop=(m == 7),
                                   r=[("sqb", s_), "cstb"], w=[PS(4)])
                            pend.append(ssq)
                            if len(pend) > 2:
                                pend.pop(0)()
                while pend:
                    pend.pop(0)()

            dma("sp", hTs[0][:], xT_own[0], ("hTl", 0), w=[("hT", 0, k) for k in range(8)])
            for t in range(NTO):
                tok = slice(t * 512, (t + 1) * 512)
                bt = t % 2
                H = hTs[bt]
                proj_add(W_OUT, lambda k: (mixF if k < 4 else mixG)[:, k % 4, tok],
                         lambda k: [("mix", k, j) for j in range(4 * t, 4 * t + 4)], H, bt, True)
                if t + 1 < NTO:
                    dma("sp", hTs[1 - bt][:], xT_own[t + 1], ("hTl", 1 - bt), w=[("hT", 1 - bt, k) for k in range(8)])
                finish_norm(G_X, H, bt)
                qb = {}

                def s1(h):
                    og = h // 2
                    if h % 2 == 0:
                        qb[og] = load_w(W_Q + og)
                    b_ = qb[og]
                    for dc in range(2):
                        m = 2 * h + dc
                        mc = m % 4
                        pbk = pbank()
                        for k in range(8):
                            mm(ps[:, pbk, :], wt[b_][:, k, mc * 128:(mc + 1) * 128], hn[:, k, :], start=(k == 0), stop=(k == 7),
                               r=[("wt", b_), ("hn", k)], w=[PS(pbk)])
                        cp("act", qraw[:, m, :], ps[:, pbk, :], r=[PS(pbk)], w=[("qraw", m)])

                def s2(h):
                    rms_rstd([(qraw[:, 2 * h + dc, :], [("qraw", 2 * h + dc)]) for dc in range(2)], 256,
                             lnbias=math.log(1.0 / 16))
                    for dc in range(2):
                        c = 2 * h + dc
                        stt(qn[:, c, :], qraw[:, c, :], gcols[:, G_XQ + dc:G_XQ + dc + 1], rstd[:], ALU.mult, ALU.mult,
                            r=[("qraw", c), "prstd", "gcols"], w=[("qn", c)])

                def s3(h):
                    for mb in range(2):
                        for dc in range(2):
                            mm(ps[:, 5 + mb, :], knT[:, 2 * h + dc, mb * 128:(mb + 1) * 128], qn[:, 2 * h + dc, :],
                               start=(dc == 0), stop=(dc == 1), r=["knT", ("qn", 2 * h + dc)], w=[PS(5 + mb)])
                        act(PTx[h % 2][mb][:], ps[:, 5 + mb, :], AF.Exp, r=[PS(5 + mb)], w=[("PTx", h % 2, mb)])

                def s4(h):
                    pt = PTx[h % 2]
                    for mb in range(2):
                        mm(ps[:, 7, :], onesb, pt[mb][:], start=(mb == 0), stop=(mb == 1),
                           r=[("PTx", h % 2, mb), "cstb"], w=[PS(7)])
                    rd = rdn[h % 2]
                    act(rd[:], ps[:, 7, :], AF.Ln, r=[PS(7)], w=[("rdn", h % 2)])
                    act(rd[:], rd[:], AF.Exp, r=[("rdn", h % 2)], w=[("rdn", h % 2)], scale=-1.0)
                    for dc in range(2):
                        c = 2 * h + dc
                        pbk = pbank()
                        for mb in range(2):
                            mm(ps[:, pbk, :], vm[:, mb, c * 128:(c + 1) * 128], pt[mb][:], start=(mb == 0), stop=(mb == 1),
                               r=["vm", ("PTx", h % 2, mb)], w=[PS(pbk)])
                        tt("dve", hn[:, c, :], ps[:, pbk, :], rd[:], ALU.mult, r=[PS(pbk), ("rdn", h % 2)], w=[("hn", c)])

                for f_, h_ in ((s1, 0), (s1, 1), (s2, 0), (s1, 2), (s2, 1), (s3, 0), (s1, 3), (s2, 2), (s3, 1), (s4, 0),
                               (s2, 3), (s3, 2), (s4, 1), (s3, 3), (s4, 2), (s4, 3)):
                    f_(h_)
                proj_add(W_O, lambda k: hn[:, k, :], lambda k: [("hn", k)], H, bt, True)
                finish_norm(G_MLP, H, bt)

                def w1s(fg):
                    uT = uTs[fg % 2]
                    for half in range(2):
                        b_ = load_w(W_1 + fg * 2 + half)
                        for mc in range(4):
                            fc = half * 4 + mc
                            pbk = pbank()
                            for k in range(8):
                                mm(ps[:, pbk, :], wt[b_][:, k, mc * 128:(mc + 1) * 128], hn[:, k, :], start=(k == 0), stop=(k == 7),
                                   r=[("wt", b_), ("hn", k)], w=[PS(pbk)])
                            ri = fc % 2
                            act(r32[ri][:], ps[:, pbk, :], AF.Relu, r=[PS(pbk)], w=[("r32", ri)])
                            tt("pool", uT[:, fc, :], r32[ri][:], r32[ri][:], ALU.mult, r=[("r32", ri)], w=[ures(fg, fc)])

                def w2s(fg):
                    uT = uTs[fg % 2]
                    for oh in range(2):
                        b_ = load_w(W_2 + fg * 2 + oh)
                        for mc in range(4):
                            m = oh * 4 + mc
                            pbk = pbank()
                            for fc in range(8):
                                mm(ps[:, pbk, :], wt[b_][:, fc, mc * 128:(mc + 1) * 128], uT[:, fc, :], start=(fc == 0), stop=(fc == 7),
                                   r=[("wt", b_), ures(fg, fc)], w=[PS(pbk)])
                            tt("dve", H[:, m, :], ps[:, pbk, :], H[:, m, :], ALU.add, r=[PS(pbk), ("hT", bt, m)], w=[("hT", bt, m)])

                for f_, g_ in ((w1s, 0), (w1s, 1), (w2s, 0), (w1s, 2), (w2s, 1), (w1s, 3), (w2s, 2), (w2s, 3)):
                    f_(g_)
                dma("pool", yT[t], H[:], ("yst", bt), r=[("hT", bt, k) for k in range(8)], w=[("yout", t)])
            P.enabled = True
            P.add("sp", lambda e_: None, [("yout", t) for t in range(NTO)], ["fin"])
            P.emit()
    return nc


def _wtile(w):
    return np.ascontiguousarray(w.reshape(8, 128, 512).transpose(1, 0, 2)).reshape(128, 4096)


def _xt_tiles(xs):
    n = xs.shape[0] // 512
    return np.ascontiguousarray(xs.reshape(n, 512, 8, 128).transpose(0, 3, 2, 1))


def prep_inputs(S, inp):
    f = lambda a: np.asarray(a, dtype=np.float32)
    x, mem = f(inp["x"]), f(inp["mem"])
    NB = S // 128
    w_in = f(inp["w_in"])
    w_ff = np.ascontiguousarray(w_in[:, 1536:1544].reshape(8, 128, 8).transpose(1, 0, 2))
    w_fox = np.stack([
        np.concatenate([w_in[:, hp * 128:(hp + 1) * 128], w_in[:, 512 + hp * 128:512 + (hp + 1) * 128],
                        w_in[:, 1024 + hp * 128:1024 + (hp + 1) * 128]], axis=1).reshape(8, 128, 384).transpose(1, 0, 2)
        for hp in range(4)])
    gq, gk, gv, glr, gr = (w_in[:, 1544:1800], w_in[:, 1800:2056], w_in[:, 2056:2568], w_in[:, 2568:2584],
                           w_in[:, 2584:3096])
    w_gla = np.ascontiguousarray(np.concatenate([gq, gk, gv, gr, glr], axis=1).reshape(8, 128, 1552).transpose(1, 0, 2))
    w_gate2a = np.zeros((33, 256), np.float32)
    w_gate2a[0:16] = f(inp["gla_w_gate2"])
    w_gate2a[32] = f(inp["gla_b_gate"])
    tiles = []
    for w in (f(inp["w_out"]), f(inp["xattn_wq"])):
        tiles += [_wtile(w[:, 0:512]), _wtile(w[:, 512:1024])]
    wkv = f(inp["xattn_wkv"])
    tiles += [_wtile(wkv[:, i * 512:(i + 1) * 512]) for i in range(4)]
    wo = f(inp["xattn_wo"])
    tiles += [_wtile(wo[:, 0:512]), _wtile(wo[:, 512:1024])]
    w1 = f(inp["mlp_w1"])
    tiles += [_wtile(w1[:, i * 512:(i + 1) * 512]) for i in range(8)]
    w2 = f(inp["mlp_w2"])
    for fg in range(4):
        for oh in range(2):
            tiles.append(_wtile(w2[fg * 1024:(fg + 1) * 1024, oh * 512:(oh + 1) * 512]))
    w_all = np.stack(tiles)
    gcols = np.zeros((128, 38), np.float32)
    for base, g in ((G_MIX, inp["norm_mix_g"]), (G_X, inp["norm_xattn_g"]), (G_MEM, inp["norm_mem_g"]),
                    (G_MLP, inp["norm_mlp_g"])):
        gcols[:, base:base + 8] = f(g).reshape(8, 128).T
    gcols[:, G_XQ:G_XQ + 2] = f(inp["xattn_q_norm_g"]).reshape(2, 128).T
    gcols[:, G_XK:G_XK + 2] = f(inp["xattn_k_norm_g"]).reshape(2, 128).T
    gcols[:, G_FQ] = np.tile(f(inp["fox_q_norm_g"]), 2)
    gcols[:, G_FK] = np.tile(f(inp["fox_k_norm_g"]), 2)
    bf_bc = np.ascontiguousarray(np.broadcast_to(np.tile(f(inp["fox_b_f"]), NB)[None, :], (128, NB * 8)))
    gout_bc = np.ascontiguousarray(np.broadcast_to(f(inp["gla_out_norm_g"])[None, :], (128, 512)))
    idx = np.arange(128)
    cst = np.zeros((128, 5, 128), np.float32)
    cst[:, 4, :] = (idx[:, None] // 64 == idx[None, :] // 64)
    cst[:, 0, :] = np.eye(128)
    cst[:, 1, :] = (idx[:, None] <= idx[None, :])
    cst[:, 2, :] = (idx[:, None] > idx[None, :])
    cst[:, 3, :] = 1.0
    causal = np.where(idx[:, None] <= idx[None, :], 0.0, NEG).astype(np.float32)
    gla_mask = np.tile((idx[:, None] <= idx[None, :]).astype(np.float32), (1, 4))
    common = dict(w_ff=w_ff, w_fox=w_fox, w_gla=w_gla, w_gate2a=w_gate2a, w_all=w_all, gcols=gcols, bf_bc=bf_bc,
                  gout_bc=gout_bc, cst=cst, gla_mask=gla_mask)
    maps = []
    for c in range(8):
        b, par = c // 2, c % 2
        xb = x[b]
        xo = xb.reshape(NB // 2, 2, 128, D)[:, par].reshape(-1, D)
        if par == 0:
            mE, mO = causal, np.full((128, 128), NEG, np.float32)
        else:
            mE, mO = np.zeros((128, 128), np.float32), causal
        m = dict(common)
        m["xT_all"] = _xt_tiles(xb)
        m["xT_own"] = _xt_tiles(xo)
        m["memT"] = np.ascontiguousarray(mem[b].reshape(256, 8, 128).transpose(2, 1, 0))
        m["mask_fox"] = np.ascontiguousarray(np.concatenate([mE, mE, mO, mO], axis=1))
        m["parf"] = np.ascontiguousarray(np.broadcast_to(np.array([par, 1 - par], np.float32)[None, :], (128, 2)))
        maps.append(m)
    return maps


def assemble(S, results, B=4):
    NB = S // 128
    out = np.zeros((B, S, D), np.float32)
    for c in range(2 * B):
        b, par = c // 2, c % 2
        yT = results[c]["yT"]
        yo = yT.transpose(0, 3, 2, 1).reshape(-1, D)
        out[b].reshape(NB // 2, 2, 128, D)[:, par] = yo.reshape(NB // 2, 128, D)
    return out


def kernel(**inputs):
    S = inputs["x"].shape[1]
    nc = build(S)
    maps = prep_inputs(S, inputs)
    res = run_bass_kernel_spmd(nc, maps, core_ids=list(range(8)))
    return assemble(S, res.results)
```

```python
import math
from contextlib import ExitStack
import numpy as np
import concourse.bass as bass
import concourse.mybir as mybir
from concourse.bass_utils import run_bass_kernel_spmd
from concourse.alu_op_type import AluOpType as ALU

F32 = mybir.dt.float32
BF16 = mybir.dt.bfloat16
AF = mybir.ActivationFunctionType

D = 1024
EPS = 1e-6
NEG = -30000.0
ENGS = ["pe", "act", "dve", "pool", "sp"]
SAME_ENGINE_SYNC = True
POOL_DMA_DEPTH = 4
STOP = 9

W_OUT, W_Q, W_KV, W_O, W_1, W_2 = 0, 2, 4, 8, 10, 18
NWT = 26
G_MIX, G_X, G_MEM, G_MLP, G_XQ, G_XK, G_FQ, G_FK = 0, 8, 16, 24, 32, 34, 36, 37


class Prog:
    def __init__(self, nc, es):
        self.nc = nc
        self.es = es
        self.ops = []
        self.last_w = {}
        self.readers = {}
        self.sems = {}
        self.emitted = 0
        self.cnt = {e: 0 for e in ENGS}
        self.dma_cnt = {}
        self.waited = {e: {} for e in ENGS}
        self.pending_barrier = {e: set() for e in ENGS}
        self.dma_since_barrier = []
        self.out_keys = {}
        self.enabled = True
        self.pool_dmas = []
        self._cap = None
        self._sink = None

    def sem(self, key):
        if key not in self.sems:
            name = "s_" + "_".join(str(k) for k in key)
            self.sems[key] = self.es.enter_context(self.nc.semaphore(name))
        return self.sems[key]

    def start_capture(self):
        self._cap = []

    def end_capture(self):
        c, self._cap = self._cap, None
        return c

    def replay(self, items):
        for it in items:
            self.add(*it)

    def sink_start(self):
        self._sink = []

    def sink_end(self):
        c, self._sink = self._sink, None
        return c

    @staticmethod
    def merge(a, b):
        out, i, j = [], 0, 0
        while i < len(a) or j < len(b):
            if j >= len(b) or (i < len(a) and i * len(b) <= j * len(a)):
                out.append(a[i]); i += 1
            else:
                out.append(b[j]); j += 1
        return out

    def add(self, eng, fn, r=(), w=(), dma=None):
        if not self.enabled:
            return None
        if self._cap is not None:
            self._cap.append((eng, fn, tuple(r), tuple(w), dma))
            return None
        if self._sink is not None:
            self._sink.append((eng, fn, tuple(r), tuple(w), dma))
            return None
        thr = None
        if eng == "pool" and dma is not None:
            n = len(self.pool_dmas)
            dma = ("pq", n % POOL_DMA_DEPTH)
            if n >= POOL_DMA_DEPTH:
                thr = self.pool_dmas[n - POOL_DMA_DEPTH]
            self.pool_dmas.append(len(self.ops))
        i = len(self.ops)
        deps = set()
        for res in r:
            if res in self.last_w:
                deps.add(self.last_w[res])
        for res in w:
            if res in self.last_w:
                deps.add(self.last_w[res])
            deps.update(self.readers.get(res, ()))
        for res in r:
            self.readers.setdefault(res, []).append(i)
        for res in w:
            self.last_w[res] = i
            self.readers[res] = []
        deps.discard(i)
        if self.pending_barrier[eng]:
            deps.update(self.pending_barrier[eng])
            self.pending_barrier[eng] = set()
        if thr is not None:
            deps.add(thr)
        self.ops.append(dict(eng=eng, fn=fn, deps=deps, dma=dma, semkey=None, val=0))
        if dma is not None:
            self.dma_since_barrier.append(i)
        return i

    def barrier(self):
        last = {}
        for i in range(len(self.ops) - 1, -1, -1):
            op = self.ops[i]
            if op["dma"] is None and op["eng"] not in last:
                last[op["eng"]] = i
            if len(last) == len(ENGS):
                break
        s = set(last.values()) | set(self.dma_since_barrier)
        self.dma_since_barrier = []
        for e in ENGS:
            self.pending_barrier[e] = set(s)

    def _skip(self, prod, cons):
        if prod["dma"] is None and cons["dma"] is None and prod["eng"] == cons["eng"]:
            if prod["eng"] == "pe":
                return True
            if not SAME_ENGINE_SYNC:
                return True
        return False

    def emit(self):
        nc = self.nc
        lo, hi = self.emitted, len(self.ops)
        needed = set()
        for i in range(lo, hi):
            op = self.ops[i]
            for d in op["deps"]:
                if d < lo or self._skip(self.ops[d], op):
                    continue
                needed.add(d)
        for i in range(lo, hi):
            op = self.ops[i]
            if op["dma"] is not None:
                k = op["dma"]
                self.dma_cnt[k] = self.dma_cnt.get(k, 0) + 1
                op["semkey"] = ("d",) + tuple(k)
                op["val"] = 16 * self.dma_cnt[k]
            elif i in needed or op.get("force"):
                self.cnt[op["eng"]] += 1
                op["semkey"] = ("e", op["eng"])
                op["val"] = self.cnt[op["eng"]]
            self_sem = op["semkey"]
            if self_sem is not None:
                self.sem(self_sem)
        handles = {}
        with nc.Block() as block:
            def mk(eng_name):
                def body(h):
                    for i in range(lo, hi):
                        op = self.ops[i]
                        if op["eng"] != eng_name:
                            continue
                        reqs = {}
                        for d in op["deps"]:
                            p = self.ops[d]
                            if self._skip(p, op):
                                continue
                            if p["semkey"] is None:
                                assert d < lo
                                continue
                            reqs[p["semkey"]] = max(reqs.get(p["semkey"], 0), p["val"])
                        wd = self.waited[eng_name]
                        for key, val in reqs.items():
                            if wd.get(key, 0) >= val:
                                continue
                            h.wait_ge(self.sems[key], val)
                            wd[key] = val
                        ins = op["fn"](h)
                        if ins is None:
                            continue
                        if op["semkey"] is not None:
                            ins.then_inc(self.sems[op["semkey"]], 16 if op["dma"] is not None else 1)
                return body
            block.tensor(mk("pe"))
            block.scalar(mk("act"))
            block.vector(mk("dve"))
            block.gpsimd(mk("pool"))
            block.sync(mk("sp"))
        self.emitted = hi

    def force_last(self):
        seen = set()
        for i in range(len(self.ops) - 1, self.emitted - 1, -1):
            op = self.ops[i]
            if op["dma"] is None and op["eng"] not in seen:
                op["force"] = True
                seen.add(op["eng"])


def build(S):
    NB = S // 128
    NJ = NB // 2
    NTA = S // 512
    NTO = S // 1024
    NCOL = NB * 8
    NG = (NCOL + 127) // 128
    GW = min(128, NCOL)
    NCOLO = NJ * 8
    NGO = (NCOLO + 127) // 128
    GWO = min(128, NCOLO)

    nc = bass.Bass("TRN2", target_bir_lowering=False)

    def din(name, shape, dt=F32):
        return nc.dram_tensor(name, list(shape), dt, kind="ExternalInput").ap()

    xT_all = din("xT_all", [NTA, 128, 8, 512])
    xT_own = din("xT_own", [NTO, 128, 8, 512])
    memT = din("memT", [128, 8, 256])
    w_ff = din("w_ff", [128, 8, 8])
    w_fox = din("w_fox", [4, 128, 8, 384])
    w_gla = din("w_gla", [128, 8, 1552])
    w_gate2a = din("w_gate2a", [33, 256])
    w_all = din("w_all", [NWT, 128, 4096])
    gcols_d = din("gcols", [128, 38])
    bf_bc_d = din("bf_bc", [128, NCOL])
    gout_bc_d = din("gout_bc", [128, 512])
    cst_d = din("cst", [128, 5, 128])
    mask_fox_d = din("mask_fox", [128, 512])
    gla_mask_d = din("gla_mask", [128, 512])
    parf_d = din("parf", [128, 2])
    yT = nc.dram_tensor("yT", [NTO, 128, 8, 512], F32, kind="ExternalOutput").ap()
    xn_all = nc.dram_tensor("xn_all", [NTA, 128, 8, 512], BF16, kind="Internal").ap()
    xn_own = nc.dram_tensor("xn_own", [NTO, 128, 8, 512], BF16, kind="Internal").ap()
    wsc = nc.dram_tensor("wsc", [NWT, 128, 4096], BF16, kind="Internal").ap()

    with ExitStack() as es:
        P = Prog(nc, es)

        def sb(name, shape, dt, stack=es):
            return stack.enter_context(nc.sbuf_tensor("sb_" + name, list(shape), dt))

        def mm(out, lhsT, rhs, start=True, stop=True, r=(), w=()):
            P.add("pe", lambda e: e.matmul(out, lhsT, rhs, start=start, stop=stop, skip_group_check=True), r, w)

        def tr(out, in_, ident, r=(), w=()):
            P.add("pe", lambda e: e.transpose(out, in_, ident), r, w)

        def act(out, in_, func, r=(), w=(), scale=1.0, bias=0.0):
            P.add("act", lambda e: e.activation(out=out, in_=in_, func=func, bias=bias, scale=scale), r, w)

        def tt(eng, out, in0, in1, op, r=(), w=()):
            P.add(eng, lambda e: e.tensor_tensor(out=out, in0=in0, in1=in1, op=op), r, w)

        def ts(eng, out, in0, s1, op0, r=(), w=(), s2=None, op1=None):
            if op1 is None:
                P.add(eng, lambda e: e.tensor_scalar(out=out, in0=in0, scalar1=s1, scalar2=None, op0=op0), r, w)
            else:
                P.add(eng, lambda e: e.tensor_scalar(out=out, in0=in0, scalar1=s1, scalar2=s2, op0=op0, op1=op1), r, w)

        def stt(out, in0, scalar, in1, op0, op1, r=(), w=()):
            P.add("dve", lambda e: e.scalar_tensor_tensor(out=out, in0=in0, scalar=scalar, in1=in1, op0=op0, op1=op1), r, w)

        def cp(eng, out, in_, r=(), w=()):
            if eng == "act":
                P.add("act", lambda e: e.activation(out=out, in_=in_, func=AF.Copy), r, w)
            else:
                P.add(eng, lambda e: e.tensor_copy(out=out, in_=in_), r, w)

        def memset(eng, ap, val, w=()):
            P.add(eng, lambda e: e.memset(ap, val), (), w)

        def dma(eng, out, in_, key, r=(), w=()):
            P.add(eng, lambda e: e.dma_start(out=out, in_=in_), r, w, dma=key)

        ps = es.enter_context(nc.psum_tensor("ps", [128, 8, 512], F32))
        pbf = ps[:, 7, :].bitcast(BF16)
        PS = lambda b: ("ps", b)

        mixG = sb("mixG", [128, 4, NJ * 128], BF16)
        gcols = sb("gcols", [128, 38], F32)
        cstf = sb("cstf", [128, 5, 128], F32)
        cstb = sb("cstb", [128, 5, 128], BF16)
        zerob = sb("zerob", [128, 256], BF16)
        parf = sb("parf", [128, 2], F32)
        maskb = sb("maskb", [128, 512], BF16)
        glamask = sb("glamask", [128, 512], BF16)
        goutbc = sb("goutbc", [128, 512], F32)
        Ck = [sb("Ck%d" % s, [128, NG, 128], BF16) for s in range(3)]
        Cq = [sb("Cq%d" % s, [128, NGO, 128], BF16) for s in range(3)]
        identf, tri_incl, tri_rev, onesf = (cstf[:, i, :] for i in range(4))
        identb, onesb, bdonesb = cstb[:, 0, :], cstb[:, 3, :], cstb[:, 4, :]

        dma("sp", gcols[:], gcols_d[:, :], ("c", 0), w=["gcols"])
        dma("sp", cstf[:], cst_d[:, :, :], ("c", 1), w=["cstf"])
        dma("pool", cstb[:], cst_d[:, :, :], ("c", 2), w=["cstb"])
        dma("sp", parf[:], parf_d[:, :], ("c", 3), w=["parf"])
        dma("pool", maskb[:], mask_fox_d[:, :], ("c", 4), w=["maskb"])
        dma("pool", glamask[:], gla_mask_d[:, :], ("c", 5), w=["glamask"])
        dma("sp", goutbc[:], gout_bc_d[:, :], ("c", 6), w=["goutbc"])
        memset("dve", zerob[:], 0.0, w=["zerob"])

        with ExitStack() as ph:
            P.enabled = STOP >= 1
            xin = [sb("xin%d" % i, [128, 8, 512], F32, ph) for i in range(2)]
            xn = [sb("xn%d" % i, [128, 8, 512], BF16, ph) for i in range(2)]
            sq = [sb("sq%d" % i, [128, 512], BF16, ph) for i in range(3)]
            lnv = sb("lnv", [128, 512], F32, ph)
            rstd = sb("rstd", [128, 512], F32, ph)
            wff = sb("wff", [128, 8, 8], BF16, ph)
            xob = [sb("xob%d" % i, [128, 8, 256], BF16, ph) for i in range(2)]
            tmpb = [sb("tmpb%d" % i, [128, 8, 128], BF16, ph) for i in range(2)]
            bfbc = sb("bfbc", [128, NCOL], F32, ph)
            L = sb("L", [128, NCOL], F32, ph)
            Lp = [sb("Lp%d" % i, [128, NJ, 8], F32, ph) for i in range(2)]
            Pex = sb("Pex", [128, NCOL], F32, ph)
            Lown = sb("Lown", [128, NCOLO], F32, ph)
            Pown = sb("Pown", [128, NCOLO], F32, ph)
            tmpo = sb("tmpo", [128, NCOLO], F32, ph)
            CT = sb("CT", [128, NG, 128], F32, ph)
            CTo = sb("CTo", [128, NGO, 128], F32, ph)
            R1 = sb("R1", [128, NG, 128], F32, ph)

            dma("pool", wff[:], w_ff[:, :, :], ("n", 0), w=["wff"])
            dma("sp", bfbc[:], bf_bc_d[:, :], ("n", 1), w=["bfbc"])
            wg = sb("wg", [128, 8, 1552], BF16, ph)
            g2a = sb("g2a", [33, 256], BF16, ph)
            glra = sb("glra", [33, 128], BF16, ph)
            glro = sb("glro", [33, 128], BF16, ph)
            ez = sb("ez", [128, 256], F32, ph)
            Gt = sb("Gt", [128, 256], F32, ph)
            E2 = sb("E2", [128, 256], F32, ph)
            kd2 = sb("kd2", [128, 256], BF16, ph)
            vb = sb("vb", [128, 512], BF16, ph)
            dec = sb("dec", [128, 4], F32, ph)
            St = [sb("St%d" % p, [128, 128], F32, ph) for p in range(2)]
            Sm = [sb("Sm%d" % p, [128, 128], F32, ph) for p in range(2)]
            Sd = [sb("Sd%d" % p, [128, 128], F32, ph) for p in range(2)]
            Sbf = [sb("Sbf%d" % p, [128, 128], BF16, ph) for p in range(2)]
            ezo = sb("ezo", [128, 256], F32, ph)
            Go = sb("Go", [128, 256], F32, ph)
            Eq = sb("Eq", [128, 256], F32, ph)
            Ek = sb("Ek", [128, 256], F32, ph)
            qdTm = [[sb("qdTm%d%d" % (p, hh), [128, 128], BF16, ph) for hh in range(2)] for p in range(2)]
            kdT = sb("kdT", [128, 2, 128], BF16, ph)
            vbo = sb("vbo", [128, 512], BF16, ph)
            ATm = sb("ATm", [128, 512], BF16, ph)
            sqo = sb("sqo", [128, 512], F32, ph)
            ss4 = sb("ss4", [128, 4], F32, ph)
            ln4 = sb("ln4", [128, 4], F32, ph)
            rs4 = sb("rs4", [128, 4], F32, ph)
            eg = sb("eg", [128, 512], F32, ph)
            sg = sb("sg", [128, 512], F32, ph)
            glo = sb("glo", [128, 512], BF16, ph)

            dma("pool", wg[:], w_gla[:, :, :], ("wg", 0), w=["wg"])
            memset("dve", g2a[:], 0.0, w=["g2a"])
            dma("pool", g2a[:], w_gate2a[:, :], ("wg", 1), w=["g2a"])
            for t_, nm in ((glra, "glra"), (glro, "glro")):
                memset("dve", t_[:], 0.0, w=[nm])
                memset("dve", t_[32:33, :], 1.0, w=[nm])
            for p in range(2):
                memset("dve", St[p][:], 0.0, w=[("St", p)])
                for hh in range(2):
                    memset("dve", qdTm[p][hh][:], 0.0, w=["qdT"])

            def gates_a(src, src_res, blk, glr_t, glr_res, gb=0):
                for k in range(8):
                    mm(ps[0:16, gb, 0:128], wg[:, k, 1536:1552], src[:, k, blk * 128:(blk + 1) * 128],
                       start=(k == 0), stop=(k == 7), r=["wg", src_res], w=[PS(gb)])
                cp("act", glr_t[0:16, :], ps[0:16, gb, 0:128], r=[PS(gb)], w=[glr_res])

            def gates_b(glr_t, glr_res, ez_t, G_t, G_res, gb=0):
                mm(ps[:, gb, 256:512], glr_t[0:33, :], g2a[0:33, :], r=[glr_res, "g2a"], w=[PS(gb)])
                act(ez_t[:], ps[:, gb, 256:512], AF.Exp, r=[PS(gb)], w=[G_res + "e"], scale=-1.0)
                act(G_t[:], ez_t[:], AF.Ln, r=[G_res + "e"], w=[G_res], bias=1.0)

            sqi = 0

            def load_tile(n):
                dma("sp", xin[n % 2][:], xT_all[n], ("xin", n % 2), w=[("xin", n % 2)])

            def n_part(T):
                nonlocal sqi
                i = T % 2
                if T + 1 < NTA:
                    load_tile(T + 1)
                for k in range(8):
                    s_ = sqi % 3
                    sqi += 1
                    act(sq[s_][:], xin[i][:, k, :], AF.Square, r=[("xin", i)], w=[("sq", s_)])
                    mm(ps[:, 6, :], onesb, sq[s_][:], start=(k == 0), stop=(k == 7),
                       r=[("sq", s_), "cstb"], w=[PS(6)])
                act(lnv[:], ps[:, 6, :], AF.Ln, r=[PS(6)], w=["lnv"], scale=1.0 / D, bias=EPS)
                act(rstd[:], lnv[:], AF.Exp, r=["lnv"], w=["rstd"], scale=-0.5)
                for k in range(8):
                    stt(xn[i][:, k, :], xin[i][:, k, :], gcols[:, G_MIX + k:G_MIX + k + 1], rstd[:],
                        ALU.mult, ALU.mult, r=[("xin", i), "rstd", "gcols"], w=[("xn", i)])
                dma("pool", xn_all[T], xn[i][:], ("xnst", i), r=[("xn", i)], w=[("xnd", "all", T)])
                for jl2 in range(2):
                    E_ = xn[i][:, :, jl2 * 256:jl2 * 256 + 128]
                    O_ = xn[i][:, :, jl2 * 256 + 128:jl2 * 256 + 256]
                    ts("dve", tmpb[jl2][:], E_, parf[:, 1:2], ALU.mult, r=[("xn", i), "parf"], w=[("tmpb", jl2)])
                    stt(xob[i][:, :, jl2 * 128:(jl2 + 1) * 128], O_, parf[:, 0:1], tmpb[jl2][:], ALU.mult, ALU.add,
                        r=[("xn", i), ("tmpb", jl2), "parf"], w=[("xob", i)])
                dma("pool", xn_own[T // 2][:, :, (T % 2) * 256:(T % 2 + 1) * 256], xob[i][:], ("xobst", i),
                    r=[("xob", i)], w=[("xnd", "own", T // 2, T % 2)])
                per = (NWT + NTA - 1) // NTA
                for wi in range(T * per, min(NWT, (T + 1) * per)):
                    dma("pool", wsc[wi], w_all[wi], ("wc", wi % 4), w=[("wsc", wi)])
                for b in range(4):
                    for k in range(8):
                        mm(ps[:, 6, b * 8:(b + 1) * 8], xn[i][:, k, b * 128:(b + 1) * 128], wff[:, k, :],
                           start=(k == 0), stop=(k == 7), r=[("xn", i), "wff"], w=[PS(6)])
                tt("dve", L[:, T * 32:(T + 1) * 32], ps[:, 6, 0:32], bfbc[:, T * 32:(T + 1) * 32], ALU.add,
                   r=[PS(6), "bfbc"], w=["L"])

            held_h1, held_h2 = [], []
            def g_tile(T):
                nonlocal held_h1, held_h2
                for b in range(4):
                    g = T * 4 + b
                    e_par = g % 2
                    j = g // 2
                    blk = slice(b * 128, (b + 1) * 128)
                    if P.enabled:
                        P.start_capture()
                    gates_a(xn[T % 2], ("xn", T % 2), b, glra, "glra")
                    for k in range(8):
                        mm(ps[:, 1, 0:256], xn[T % 2][:, k, blk], wg[:, k, 256:512], start=(k == 0), stop=(k == 7),
                           r=["wg", ("xn", T % 2)], w=[PS(1)])
                    gates_b(glra, "glra", ez, Gt, "Gt")
                    for k in range(8):
                        mm(ps[:, 2, :], xn[T % 2][:, k, blk], wg[:, k, 512:1024], start=(k == 0), stop=(k == 7),
                           r=["wg", ("xn", T % 2)], w=[PS(2)])
                    cp("act", vb[:], ps[:, 2, :], r=[PS(2)], w=["vb"])
                    mm(ps[:, 3, 0:256], tri_rev, Gt[:], r=["Gt", "cstf"], w=[PS(3)])
                    for p in range(2):
                        mm(ps[:, 3, 256 + 2 * p:258 + 2 * p], Gt[:, p * 128:(p + 1) * 128], onesf[:, 0:2],
                           start=False, stop=True, r=["Gt", "cstf"], w=[PS(3)])
                    act(E2[:], ps[:, 3, 0:256], AF.Exp, r=[PS(3)], w=["E2"], scale=-1.0 / 16)
                    act(dec[:], ps[:, 3, 256:260], AF.Exp, r=[PS(3)], w=["dec"], scale=-1.0 / 16)
                    tt("dve", kd2[:], ps[:, 1, 0:256], E2[:], ALU.mult, r=[PS(1), "E2"], w=["kd2"])
                    for h in range(4):
                        mm(ps[(h % 2) * 64:(h % 2) * 64 + 64, 1, 256 + (h // 2) * 128:256 + (h // 2 + 1) * 128],
                           kd2[:, h * 64:(h + 1) * 64], vb[:, h * 128:(h + 1) * 128],
                           r=["kd2", "vb"], w=[PS(1)])
                    for p in range(2):
                        dsv = ps[:, 1, 256 + p * 128:256 + (p + 1) * 128]
                        if e_par == 0:
                            stt(Sm[p][:], St[p][:], dec[:, 2 * p:2 * p + 1], dsv, ALU.mult, ALU.add,
                                r=[("St", p), "dec", PS(1)], w=[("Sm", p)])
                        else:
                            stt(St[p][:], Sm[p][:], dec[:, 2 * p:2 * p + 1], dsv, ALU.mult, ALU.add,
                                r=[("Sm", p), "dec", PS(1)], w=[("St", p)])
                    if e_par != 0:
                        if P.enabled:
                            capB = P.end_capture()
                            P.replay(Prog.merge(held_h1, capB))
                        continue
                    if P.enabled:
                        capB = P.end_capture()
                        P.replay(Prog.merge(held_h2, capB))
                        held_h2 = []
                    for p in range(2):
                        tt("dve", Sd[p][:], Sm[p][:], St[p][:], ALU.subtract, r=[("Sm", p), ("St", p)], w=[("Sd", p)])
                        stt(Sbf[p][:], Sd[p][:], parf[:, 0:1], St[p][:], ALU.mult, ALU.add,
                            r=[("Sd", p), ("St", p), "parf"], w=[("Sbf", p)])
                    if P.enabled:
                        P.start_capture()
                    jl = j % 2
                    oblk = slice(jl * 128, (jl + 1) * 128)
                    xres = ("xob", T % 2)
                    gates_a(xob[T % 2], xres, jl, glro, "glro", gb=7)
                    for k in range(8):
                        mm(ps[:, 5, :], xob[T % 2][:, k, oblk], wg[:, k, 1024:1536], start=(k == 0), stop=(k == 7),
                           r=["wg", xres], w=[PS(5)])
                    gates_b(glro, "glro", ezo, Go, "Go", gb=7)
                    act(eg[:], ps[:, 5, :], AF.Exp, r=[PS(5)], w=["eg"], scale=-1.0)
                    act(eg[:], eg[:], AF.Ln, r=["eg"], w=["eg"], bias=1.0)
                    act(eg[:], eg[:], AF.Exp, r=["eg"], w=["eg"], scale=-1.0)
                    tt("dve", sg[:], ps[:, 5, :], eg[:], ALU.mult, r=[PS(5), "eg"], w=["sg"])
                    tt("pool", sg[:], sg[:], goutbc[:], ALU.mult, r=["sg", "goutbc"], w=["sg"])
                    for qi in range(4):
                        c0 = qi * 128
                        for k in range(8):
                            mm(ps[:, 4, qi * 128:(qi + 1) * 128], wg[:, k, c0:c0 + 128], xob[T % 2][:, k, oblk],
                               start=(k == 0), stop=(k == 7), r=["wg", xres], w=[PS(4)])
                    for k in range(8):
                        mm(ps[:, 5, :], xob[T % 2][:, k, oblk], wg[:, k, 512:1024], start=(k == 0), stop=(k == 7),
                           r=["wg", xres], w=[PS(5)])
                    cp("act", vbo[:], ps[:, 5, :], r=[PS(5)], w=["vbo"])
                    for p in range(2):
                        mm(ps[:, 7, p * 128:(p + 1) * 128], Go[:, p * 128:(p + 1) * 128], tri_incl,
                           r=["Go", "cstf"], w=[PS(7)])
                    act(Eq[:], ps[:, 7, 0:256], AF.Exp, r=[PS(7)], w=["Eq"], scale=-1.0 / 16)
                    act(Ek[:], ps[:, 7, 0:256], AF.Exp, r=[PS(7)], w=["Ek"], scale=1.0 / 16)
                    for p in range(2):
                        for hh in range(2):
                            rows = slice(hh * 64, hh * 64 + 64)
                            stt(qdTm[p][hh][rows, :], ps[rows, 4, p * 128:(p + 1) * 128], 0.125,
                                Eq[rows, p * 128:(p + 1) * 128], ALU.mult, ALU.mult, r=[PS(4), "Eq"], w=["qdT"])
                    tt("dve", kdT[:].rearrange("p a b -> p (a b)"), ps[:, 4, 256:512], Ek[:], ALU.mult,
                       r=[PS(4), "Ek"], w=["kdT"])
                    if P.enabled:
                        held_h1 = P.end_capture()
                        P.start_capture()
                    for h in range(4):
                        pr = slice((h % 2) * 64, (h % 2) * 64 + 64)
                        mm(ps[:, 5, h * 128:(h + 1) * 128], kdT[:, h // 2, :], qdTm[h // 2][h % 2][:],
                           r=["kdT", "qdT"], w=[PS(5)])
                    tt("dve", ATm[:], ps[:, 5, :], glamask[:], ALU.mult, r=[PS(5), "glamask"], w=["ATm"])
                    for h in range(4):
                        pr = slice((h % 2) * 64, (h % 2) * 64 + 64)
                        mm(ps[:, 4, h * 128:(h + 1) * 128], ATm[:, h * 128:(h + 1) * 128], vbo[:, h * 128:(h + 1) * 128],
                           start=True, stop=False, r=["ATm", "vbo"], w=[PS(4)])
                        mm(ps[:, 4, h * 128:(h + 1) * 128], qdTm[h // 2][h % 2][:], Sbf[h // 2][:],
                           start=False, stop=True, r=["qdT", ("Sbf", h // 2)], w=[PS(4)])
                    act(sqo[:], ps[:, 4, :], AF.Square, r=[PS(4)], w=["sqo"])
                    P.add("dve", lambda e_, o_=ss4[:], i_=sqo[:].rearrange("p (h d) -> p h d", h=4):
                          e_.tensor_reduce(out=o_, in_=i_, axis=mybir.AxisListType.X, op=ALU.add), ["sqo"], ["ss4"])
                    act(ln4[:], ss4[:], AF.Ln, r=["ss4"], w=["ln4"], scale=1.0 / 128, bias=EPS)
                    act(rs4[:], ln4[:], AF.Exp, r=["ln4"], w=["rs4"], scale=-0.5)
                    for h in range(4):
                        stt(glo[:, h * 128:(h + 1) * 128], ps[:, 4, h * 128:(h + 1) * 128], rs4[:, h:h + 1],
                            sg[:, h * 128:(h + 1) * 128], ALU.mult, ALU.mult, r=[PS(4), "rs4", "sg"], w=["glo"])
                    for h in range(4):
                        tr(pbf[:, h * 128:(h + 1) * 128], glo[:, h * 128:(h + 1) * 128], identb, r=["glo", "cstb"], w=[PS(7)])
                    cp("act", mixG[:, 0:4, j * 128:(j + 1) * 128], pbf[:, 0:512].rearrange("p (c t) -> p c t", c=4),
                       r=[PS(7)], w=[("mix", 4 + c, j) for c in range(4)])
                    if P.enabled:
                        held_h2 = P.end_capture()

            load_tile(0)
            n_part(0)
            for T in range(NTA):
                capN = []
                if T + 1 < NTA and P.enabled:
                    P.start_capture()
                    n_part(T + 1)
                    capN = P.end_capture()
                if P.enabled:
                    P.sink_start()
                g_tile(T)
                if P.enabled:
                    tile_ops = P.sink_end()
                    cut = int(0.55 * len(tile_ops))
                    P.replay(Prog.merge(tile_ops[:cut], capN) + tile_ops[cut:])
            if P.enabled:
                P.replay(held_h2)
            act(L[:], L[:], AF.Exp, r=["L"], w=["L"], scale=-1.0)
            act(L[:], L[:], AF.Ln, r=["L"], w=["L"], bias=1.0)
            L4 = L[:].rearrange("p (j e h) -> p j e h", e=2, h=8)
            Pex4 = Pex[:].rearrange("p (j e h) -> p j e h", e=2, h=8)
            tt("dve", Lp[0][:], L4[:, :, 0, :], L4[:, :, 1, :], ALU.add, r=["L"], w=[("Lp", 0)])
            cur = 0
            s_ = 1
            while s_ < NJ:
                nxt = 1 - cur
                tt("dve", Lp[nxt][:, s_:, :], Lp[cur][:, s_:, :], Lp[cur][:, :NJ - s_, :], ALU.add,
                   r=[("Lp", cur)], w=[("Lp", nxt)])
                cp("dve", Lp[nxt][:, :s_, :], Lp[cur][:, :s_, :], r=[("Lp", cur)], w=[("Lp", nxt)])
                cur = nxt
                s_ *= 2
            tt("dve", Pex4[:, :, 0, :], Lp[cur][:], L4[:, :, 0, :], ALU.subtract, r=[("Lp", cur), "L"], w=["Pex"])
            tt("dve", Pex4[:, :, 0, :], Pex4[:, :, 0, :], L4[:, :, 1, :], ALU.subtract, r=["Pex", "L"], w=["Pex"])
            tt("dve", Pex4[:, :, 1, :], Pex4[:, :, 0, :], L4[:, :, 0, :], ALU.add, r=["Pex", "L"], w=["Pex"])
            Lo3 = Lown[:].rearrange("p (j h) -> p j h", h=8)
            Po3 = Pown[:].rearrange("p (j h) -> p j h", h=8)
            To3 = tmpo[:].rearrange("p (j h) -> p j h", h=8)
            ts("dve", To3, L4[:, :, 1, :], parf[:, 0:1], ALU.mult, r=["L", "parf"], w=["tmpo"])
            stt(Lo3, L4[:, :, 0, :], parf[:, 1:2], To3, ALU.mult, ALU.add, r=["L", "tmpo", "parf"], w=["Lown"])
            ts("dve", To3, Pex4[:, :, 1, :], parf[:, 0:1], ALU.mult, r=["Pex", "parf", "Lown"], w=["tmpo"])
            stt(Po3, Pex4[:, :, 0, :], parf[:, 1:2], To3, ALU.mult, ALU.add, r=["Pex", "tmpo", "parf"], w=["Pown"])
            for G in range(NG):
                mm(ps[0:GW, 2, G * 128:(G + 1) * 128], L[:, G * GW:(G + 1) * GW], tri_incl, start=True, stop=False,
                   r=["L", "cstf"], w=[PS(2)])
                mm(ps[0:GW, 2, G * 128:(G + 1) * 128], Pex[:, G * GW:(G + 1) * GW], onesf, start=False, stop=True,
                   r=["Pex", "cstf"], w=[PS(2)])
            for G in range(NGO):
                mm(ps[0:GWO, 3, G * 128:(G + 1) * 128], Lown[:, G * GWO:(G + 1) * GWO], tri_incl, start=True, stop=False,
                   r=["Lown", "cstf"], w=[PS(3)])
                mm(ps[0:GWO, 3, G * 128:(G + 1) * 128], Pown[:, G * GWO:(G + 1) * GWO], onesf, start=False, stop=True,
                   r=["Pown", "cstf"], w=[PS(3)])
            CTf = CT[:].rearrange("p g t -> p (g t)")
            CTof = CTo[:].rearrange("p g t -> p (g t)")
            R1f = R1[:].rearrange("p g t -> p (g t)")
            cp("act", CTf[0:GW, :], ps[0:GW, 2, 0:NG * 128], r=[PS(2)], w=["CT"])
            ts("dve", CTof[0:GWO, :], ps[0:GWO, 3, 0:NGO * 128], -1.0, ALU.mult, r=[PS(3)], w=["CTo"])
            for (src, dstl, gw, ng, nm) in ((CTf, Ck, GW, NG, "CT"), (CTof, Cq, GWO, NGO, "CTo")):
                d0 = dstl[0][:].rearrange("p g t -> p (g t)")
                d1 = dstl[1][:].rearrange("p g t -> p (g t)")
                d2 = dstl[2][:].rearrange("p g t -> p (g t)")
                n = ng * 128
                cp("dve", d0[0:gw, :], src[0:gw, :], r=[nm], w=[("C", nm, 0)])
                tt("dve", R1f[0:gw, 0:n], src[0:gw, :], d0[0:gw, :], ALU.subtract, r=[nm, ("C", nm, 0)], w=["R1"])
                cp("dve", d1[0:gw, :], R1f[0:gw, 0:n], r=["R1"], w=[("C", nm, 1)])
                tt("dve", R1f[0:gw, 0:n], R1f[0:gw, 0:n], d1[0:gw, :], ALU.subtract, r=["R1", ("C", nm, 1)], w=["R1"])
                cp("dve", d2[0:gw, :], R1f[0:gw, 0:n], r=["R1"], w=[("C", nm, 2)])
            P.force_last()
            P.barrier()
            P.emit()

        mixF = sb("mixF", [128, 4, NJ * 128], BF16)
        with ExitStack() as ph:
            P.enabled = STOP >= 2
            KT = [sb("KT%d" % h, [128, S], BF16, ph) for h in range(2)]
            QT = [sb("QT%d" % h, [128, NJ * 128], BF16, ph) for h in range(2)]
            Vaug = sb("Vaug", [128, NB, 2, 65], BF16, ph)
            wf = sb("wf", [128, 8, 384], BF16, ph)
            xa = [sb("xa%d" % i, [128, 8, 512], BF16, ph) for i in range(3)]
            xo = [sb("xo%d" % i, [128, 8, 512], BF16, ph) for i in range(2)]
            sqk = [sb("sqk%d" % i, [128, 512], BF16, ph) for i in range(3)]
            rsk = [sb("rsk%d" % i, [128, 512], F32, ph) for i in range(3)]
            PSETS = [(4, 5), (0, 1), (3, 2)]
            NPT = 6
            PT = [sb("PT%d" % i, [128, 512], BF16, ph) for i in range(NPT)]
            fo = [sb("fo%d" % i, [128, 128], BF16, ph) for i in range(2)]
            rdens = [sb("rdenf%d" % i, [128, 2, 1], F32, ph) for i in range(2)]
            junk = sb("junk", [128, 2], F32, ph)
            AUG0 = [64, 0]
            KROWS = [70, 128]
            SBK = [0, 1, 3, 5, 6, 4]
            ACCB = 2

            memset("pool", Vaug[:, :, :, 64:65], 1.0, w=["Vones"])
            memset("pool", KT[1][0:64, :], 0.0, w=[("KTaugm", 1), ("KTaugR", 1)])
            memset("pool", QT[1][0:64, :], 0.0, w=[("QTaugm", 1), ("QTaugR", 1)])
            xa_i = 0
            xo_i = 0
            sq_i = 0
            pt_i = 0
            ss_i = 0
            fo_i = 0
            pending_tail = [None]
            for hp in range(4):
                dma("pool", wf[:], w_fox[hp], ("wf", 0), w=["wf"])
                for hh in range(2):
                    h = 2 * hp + hh
                    a0 = AUG0[hh]
                    if hp == 0:
                        memset("dve", KT[hh][a0:a0 + 6, :], 1.0, w=[("KTaugm", hh), ("KTaugR", hh)])
                        memset("dve", QT[hh][a0:a0 + 6, :], 1.0, w=[("QTaugm", hh), ("QTaugR", hh)])
                    else:
                        memset("pool", junk[:, 0:1], 0.0,
                               w=[("KTaugm", hh), ("KTaugR", hh), ("QTaugm", hh), ("QTaugR", hh), "junk"])
                    kres, qres = [], []
                    for s_ in range(3):
                        for G in range(NG):
                            nblk = GW // 8
                            dma("pool", KT[hh][a0 + 3 + s_:a0 + 4 + s_, G * nblk * 128:(G + 1) * nblk * 128],
                                Ck[s_][h:GW:8, G, :], ("aug", 0),
                                r=[("C", "CT", s_), ("KTaugm", hh)], w=[("KTaugd", hh, s_, G)])
                            kres.append(("KTaugd", hh, s_, G))
                        for G in range(NGO):
                            nblk = GWO // 8
                            dma("pool", QT[hh][a0 + s_:a0 + 1 + s_, G * nblk * 128:(G + 1) * nblk * 128],
                                Cq[s_][h:GWO:8, G, :], ("aug", 1),
                                r=[("C", "CTo", s_), ("QTaugm", hh)], w=[("QTaugd", hh, s_, G)])
                            qres.append(("QTaugd", hh, s_, G))
                    memset("pool", junk[:, 0:1], 0.0, w=[("KTaugR", hh), "junk"])
                    if P.enabled:
                        P.ops[-1]["deps"].update(P.last_w[r_] for r_ in kres)
                    memset("pool", junk[:, 1:2], 0.0, w=[("QTaugR", hh), "junk"])
                    if P.enabled:
                        P.ops[-1]["deps"].update(P.last_w[r_] for r_ in qres)

                def proj_norm2(src, src_res, col0, dsts, cols, dst_res_fn, gcol, lnbias):
                    nonlocal sq_i
                    s_ = sq_i % 3
                    sq_i += 1
                    pb, qb_ = PSETS[s_]
                    for k in range(8):
                        mm(ps[:, pb, :], wf[:, k, col0:col0 + 128], src[:, k, :],
                           start=(k == 0), stop=(k == 7), r=["wf", src_res], w=[PS(pb)])
                    act(sqk[s_][:], ps[:, pb, :], AF.Square, r=[PS(pb)], w=[("sqk", s_)])
                    mm(ps[:, qb_, :], bdonesb, sqk[s_][:], r=[("sqk", s_), "cstb"], w=[PS(qb_)])
                    act(rsk[s_][:], ps[:, qb_, :], AF.Ln, r=[PS(qb_)], w=[("rsk", s_)], scale=1.0 / 64, bias=EPS)
                    act(rsk[s_][:], rsk[s_][:], AF.Exp, r=[("rsk", s_)], w=[("rsk", s_)], scale=-0.5, bias=lnbias)
                    for hh in range(2):
                        rows = slice(hh * 64, hh * 64 + 64)
                        stt(dsts[hh][rows, cols], ps[rows, pb, :], gcols[rows, gcol:gcol + 1], rsk[s_][rows, :],
                            ALU.mult, ALU.mult, r=[PS(pb), ("rsk", s_), "gcols"], w=[dst_res_fn(hh)])

                def proj_tile(T):
                    nonlocal xa_i, xo_i
                    a = xa_i % 3
                    xa_i += 1
                    dma("sp", xa[a][:], xn_all[T], ("xa", a), r=[("xnd", "all", T)], w=[("xa", a)])
                    proj_norm2(xa[a], ("xa", a), 128, KT, slice(T * 512, (T + 1) * 512),
                               lambda hh, T=T: ("KT", hh, T), G_FK, 0.0)
                    for b in range(4):
                        for k in range(8):
                            mm(ps[:, 6, b * 128:(b + 1) * 128], xa[a][:, k, b * 128:(b + 1) * 128], wf[:, k, 256:384],
                               start=(k == 0), stop=(k == 7), r=["wf", ("xa", a)], w=[PS(6)])
                    cp("dve", Vaug[:, T * 4:(T + 1) * 4, :, 0:64],
                       ps[:, 6, :].rearrange("p (b h d) -> p b h d", b=4, h=2), r=[PS(6)], w=[("V", T)])
                    if T % 2 == 0:
                        t = T // 2
                        o = xo_i % 2
                        xo_i += 1
                        dma("sp", xo[o][:], xn_own[t], ("xo", o), r=[("xnd", "own", t, 0), ("xnd", "own", t, 1)], w=[("xo", o)])
                        proj_norm2(xo[o], ("xo", o), 0, QT, slice(t * 512, (t + 1) * 512),
                                   lambda hh, t=t: ("QT", hh, t), G_FQ, math.log(0.125))

                for T in range(NTA):
                    proj_tile(T)
                steps_l = [(j, kp) for j in range(NJ) for kp in range(j + 1)]

                def s_mm(idx):
                    nonlocal ss_i
                    j, kp = steps_l[idx]
                    t = j // 4
                    sbk = SBK[ss_i % 6]
                    ss_i += 1
                    diag = (kp == j)
                    if diag:
                        mm(ps[:, sbk, :], identb, maskb[:], start=True, stop=False, r=["cstb", "maskb"], w=[PS(sbk)])
                    for e in range(2):
                        for hh in range(2):
                            kb = 2 * kp + e
                            kr = KROWS[hh]
                            mm(ps[:, sbk, (e * 2 + hh) * 128:(e * 2 + hh + 1) * 128],
                               KT[hh][0:kr, kb * 128:(kb + 1) * 128], QT[hh][0:kr, j * 128:(j + 1) * 128],
                               start=(not diag), stop=True,
                               r=[("KT", hh, kb // 4), ("KTaugR", hh), ("QT", hh, t), ("QTaugR", hh)], w=[PS(sbk)])
                    return sbk

                q_ = [s_mm(i_) for i_ in range(min(5, len(steps_l)))]
                for idx, (j, kp) in enumerate(steps_l):
                    ab = ACCB
                    sbk = q_.pop(0)
                    if idx + 5 < len(steps_l):
                        q_.append(s_mm(idx + 5))
                    if kp == 0:
                        if pending_tail[0] is not None:
                            pending_tail[0]()
                            pending_tail[0] = None
                        mm(ps[:, ab, 0:130], zerob[:, 0:128], zerob[:, 0:130], start=True, stop=False, r=["zerob"], w=[PS(ab)])
                    p_ = pt_i % NPT
                    pt_i += 1
                    act(PT[p_][:], ps[:, sbk, :], AF.Exp, r=[PS(sbk)], w=[("PT", p_)])
                    for e in range(2):
                        for hh in range(2):
                            kb = 2 * kp + e
                            mm(ps[:, ab, hh * 65:(hh + 1) * 65], PT[p_][:, (e * 2 + hh) * 128:(e * 2 + hh + 1) * 128],
                               Vaug[:, kb, hh, :], start=False, stop=(kp == j and e == 1),
                               r=[("PT", p_), ("V", kb // 4), "Vones"], w=[PS(ab)])
                    if kp != j:
                        continue
                    acc3 = ps[:, ab, 0:130].rearrange("p (h d) -> p h d", h=2)
                    rden = rdens[j % 2]
                    P.add("dve", lambda e_, o_=rden[:], i_=acc3[:, :, 64:65]: e_.reciprocal(out=o_, in_=i_),
                          [PS(ab)], [("rdenf", j % 2)])
                    f_ = fo_i % 2
                    fo_i += 1
                    for hh in range(2):
                        ts("dve", fo[f_][:, hh * 64:(hh + 1) * 64], ps[:, ab, hh * 65:hh * 65 + 64], rden[:, hh, :], ALU.mult,
                           r=[PS(ab), ("rdenf", j % 2)], w=[("fo", f_)])

                    def tail(f_=f_, hp=hp, j=j):
                        tr(pbf[:, 0:128], fo[f_][:], identb, r=[("fo", f_), "cstb"], w=[PS(7)])
                        cp("dve", mixF[:, hp, j * 128:(j + 1) * 128], pbf[:, 0:128], r=[PS(7)], w=[("mix", hp, j)])
                    pending_tail[0] = tail
            if pending_tail[0] is not None:
                pending_tail[0]()
                pending_tail[0] = None
            P.force_last()
            P.barrier()
            P.emit()

        with ExitStack() as ph:
            P.enabled = STOP >= 4
            NWB = 3
            wt = [sb("wt%d" % i, [128, 8, 512], BF16, ph) for i in range(NWB)]
            hT = sb("hT", [128, 8, 512], F32, ph)
            hT2 = sb("hT2", [128, 8, 512], F32, ph)
            uT2 = sb("uT2", [128, 8, 512], BF16, ph)
            hn = sb("hn", [128, 8, 512], BF16, ph)
            sqb = [sb("sqb%d" % i, [128, 512], BF16, ph) for i in range(4)]
            lnv = sb("plnv", [128, 512], F32, ph)
            rstd = sb("prstd", [128, 512], F32, ph)
            qraw = sb("qraw", [128, 8, 512], F32, ph)
            qn = sb("qn", [128, 8, 512], BF16, ph)
            PTx = [[sb("PTx%d%d" % (a_, i), [128, 512], BF16, ph) for i in range(2)] for a_ in range(2)]
            rdn = [sb("rdn%d" % i, [128, 512], F32, ph) for i in range(2)]
            r32 = [sb("r32%d" % i, [128, 512], F32, ph) for i in range(2)]
            knT = sb("knT", [128, 8, 256], BF16, ph)
            vm = sb("vm", [128, 2, 1024], BF16, ph)
            memn = sb("memn", [128, 8, 256], BF16, ph)

            wt_i = [0]

            def load_w(idx):
                b = wt_i[0] % NWB
                wt_i[0] += 1
                dma("sp", wt[b][:].rearrange("p k n -> p (k n)"), wsc[idx], ("wt", b), r=[("wsc", idx)], w=[("wt", b)])
                return b

            pb_i = [0]

            def pbank():
                b = pb_i[0] % 4
                pb_i[0] += 1
                return b

            sq_i = [0]

            def nsq():
                s_ = sq_i[0] % 4
                sq_i[0] += 1
                return s_

            def rms_rstd(srcs, n_feat, lnbias=0.0, npart=128, ncol=512):
                n = len(srcs)
                for i, (ap, res) in enumerate(srcs):
                    s_ = nsq()
                    act(sqb[s_][:, 0:ncol], ap, AF.Square, r=res, w=[("sqb", s_)])
                    mm(ps[:, 4, 0:ncol], onesb, sqb[s_][:, 0:ncol], start=(i == 0), stop=(i == n - 1),
                       r=[("sqb", s_), "cstb"], w=[PS(4)])
                act(lnv[:, 0:ncol], ps[:, 4, 0:ncol], AF.Ln, r=[PS(4)], w=["plnv"], scale=1.0 / n_feat, bias=EPS)
                act(rstd[:, 0:ncol], lnv[:, 0:ncol], AF.Exp, r=["plnv"], w=["prstd"], scale=-0.5, bias=lnbias)

            dma("sp", qraw[:, :, 0:256], memT[:, :, :], ("mem", 0), w=["qraw"])
            rms_rstd([(qraw[:, k, 0:256], ["qraw"]) for k in range(8)], D, ncol=256)
            for k in range(8):
                stt(memn[:, k, :], qraw[:, k, 0:256], gcols[:, G_MEM + k:G_MEM + k + 1], rstd[:, 0:256], ALU.mult, ALU.mult,
                    r=["qraw", "prstd", "gcols"], w=["memn"])
            for og in range(2):
                b = load_w(W_KV + og)
                for mc in range(4):
                    c = og * 4 + mc
                    pbk = pbank()
                    for k in range(8):
                        mm(ps[:, pbk, 0:256], wt[b][:, k, mc * 128:(mc + 1) * 128], memn[:, k, :], start=(k == 0), stop=(k == 7),
                           r=[("wt", b), "memn"], w=[PS(pbk)])
                    cp("act", hT[:, c, 0:256], ps[:, pbk, 0:256], r=[PS(pbk)], w=[("hT", 0, c)])
            for h in range(4):
                rms_rstd([(hT[:, 2 * h + dc, 0:256], [("hT", 0, 2 * h + dc)]) for dc in range(2)], 256, ncol=256)
                for dc in range(2):
                    c = 2 * h + dc
                    stt(knT[:, c, :], hT[:, c, 0:256], gcols[:, G_XK + dc:G_XK + dc + 1], rstd[:, 0:256], ALU.mult, ALU.mult,
                        r=[("hT", 0, c), "prstd", "gcols"], w=["knT"])
            for og in range(2):
                b = load_w(W_KV + 2 + og)
                for mb in range(2):
                    pbk = pbank()
                    for k in range(8):
                        mm(ps[:, pbk, :], memn[:, k, mb * 128:(mb + 1) * 128], wt[b][:, k, :], start=(k == 0), stop=(k == 7),
                           r=[("wt", b), "memn"], w=[PS(pbk)])
                    cp("act", vm[:, mb, og * 512:(og + 1) * 512], ps[:, pbk, :], r=[PS(pbk)], w=["vm"])

            hTs = [hT, hT2]
            uTs = [qn, uT2]
            ures = lambda fg, fc: ("qn", fc) if fg % 2 == 0 else ("uT2", fc)

            def finish_norm(gbase, H, bt):
                act(lnv[:], ps[:, 4, :], AF.Ln, r=[PS(4)], w=["plnv"], scale=1.0 / D, bias=EPS)
                act(rstd[:], lnv[:], AF.Exp, r=["plnv"], w=["prstd"], scale=-0.5)
                for k in range(8):
                    stt(hn[:, k, :], H[:, k, :], gcols[:, gbase + k:gbase + k + 1], rstd[:], ALU.mult, ALU.mult,
                        r=[("hT", bt, k), "prstd", "gcols"], w=[("hn", k)])

            def proj_add(widx, src_fn, src_res_fn, H, bt, sumsq):
                pend = []
                for og in range(2):
                    b_ = load_w(widx + og)
                    for mc in range(4):
                        m = og * 4 + mc
                        pbk = pbank()
                        for k in range(8):
                            mm(ps[:, pbk, :], wt[b_][:, k, mc * 128:(mc + 1) * 128], src_fn(k), start=(k == 0), stop=(k == 7),
                               r=[("wt", b_)] + src_res_fn(k), w=[PS(pbk)])
                        tt("dve", H[:, m, :], ps[:, pbk, :], H[:, m, :], ALU.add, r=[PS(pbk), ("hT", bt, m)], w=[("hT", bt, m)])
                        if sumsq:
                            s_ = nsq()
                            act(sqb[s_][:], H[:, m, :], AF.Square, r=[("hT", bt, m)], w=[("sqb", s_)])

                            def ssq(m=m, s_=s_):
                                mm(ps[:, 4, :], onesb, sqb[s_][:], start=(m == 0), stop=(m == 7),
                                   r=[("sqb", s_), "cstb"], w=[PS(4)])
                            pend.append(ssq)
                            if len(pend) > 2:
                                pend.pop(0)()
                while pend:
                    pend.pop(0)()

            dma("sp", hTs[0][:], xT_own[0], ("hTl", 0), w=[("hT", 0, k) for k in range(8)])
            for t in range(NTO):
                tok = slice(t * 512, (t + 1) * 512)
                bt = t % 2
                H = hTs[bt]
                proj_add(W_OUT, lambda k: (mixF if k < 4 else mixG)[:, k % 4, tok],
                         lambda k: [("mix", k, j) for j in range(4 * t, 4 * t + 4)], H, bt, True)
                if t + 1 < NTO:
                    dma("sp", hTs[1 - bt][:], xT_own[t + 1], ("hTl", 1 - bt), w=[("hT", 1 - bt, k) for k in range(8)])
                finish_norm(G_X, H, bt)
                qb = {}

                def s1(h):
                    og = h // 2
                    if h % 2 == 0:
                        qb[og] = load_w(W_Q + og)
                    b_ = qb[og]
                    for dc in range(2):
                        m = 2 * h + dc
                        mc = m % 4
                        pbk = pbank()
                        for k in range(8):
                            mm(ps[:, pbk, :], wt[b_][:, k, mc * 128:(mc + 1) * 128], hn[:, k, :], start=(k == 0), stop=(k == 7),
                               r=[("wt", b_), ("hn", k)], w=[PS(pbk)])
                        cp("act", qraw[:, m, :], ps[:, pbk, :], r=[PS(pbk)], w=[("qraw", m)])

                def s2(h):
                    rms_rstd([(qraw[:, 2 * h + dc, :], [("qraw", 2 * h + dc)]) for dc in range(2)], 256,
                             lnbias=math.log(1.0 / 16))
                    for dc in range(2):
                        c = 2 * h + dc
                        stt(qn[:, c, :], qraw[:, c, :], gcols[:, G_XQ + dc:G_XQ + dc + 1], rstd[:], ALU.mult, ALU.mult,
                            r=[("qraw", c), "prstd", "gcols"], w=[("qn", c)])

                def s3(h):
                    for mb in range(2):
                        for dc in range(2):
                            mm(ps[:, 5 + mb, :], knT[:, 2 * h + dc, mb * 128:(mb + 1) * 128], qn[:, 2 * h + dc, :],
                               start=(dc == 0), stop=(dc == 1), r=["knT", ("qn", 2 * h + dc)], w=[PS(5 + mb)])
                        act(PTx[h % 2][mb][:], ps[:, 5 + mb, :], AF.Exp, r=[PS(5 + mb)], w=[("PTx", h % 2, mb)])

                def s4(h):
                    pt = PTx[h % 2]
                    for mb in range(2):
                        mm(ps[:, 7, :], onesb, pt[mb][:], start=(mb == 0), stop=(mb == 1),
                           r=[("PTx", h % 2, mb), "cstb"], w=[PS(7)])
                    rd = rdn[h % 2]
                    act(rd[:], ps[:, 7, :], AF.Ln, r=[PS(7)], w=[("rdn", h % 2)])
                    act(rd[:], rd[:], AF.Exp, r=[("rdn", h % 2)], w=[("rdn", h % 2)], scale=-1.0)
                    for dc in range(2):
                        c = 2 * h + dc
                        pbk = pbank()
                        for mb in range(2):
                            mm(ps[:, pbk, :], vm[:, mb, c * 128:(c + 1) * 128], pt[mb][:], start=(mb == 0), stop=(mb == 1),
                               r=["vm", ("PTx", h % 2, mb)], w=[PS(pbk)])
                        tt("dve", hn[:, c, :], ps[:, pbk, :], rd[:], ALU.mult, r=[PS(pbk), ("rdn", h % 2)], w=[("hn", c)])

                for f_, h_ in ((s1, 0), (s1, 1), (s2, 0), (s1, 2), (s2, 1), (s3, 0), (s1, 3), (s2, 2), (s3, 1), (s4, 0),
                               (s2, 3), (s3, 2), (s4, 1), (s3, 3), (s4, 2), (s4, 3)):
                    f_(h_)
                proj_add(W_O, lambda k: hn[:, k, :], lambda k: [("hn", k)], H, bt, True)
                finish_norm(G_MLP, H, bt)

                def w1s(fg):
                    uT = uTs[fg % 2]
                    for half in range(2):
                        b_ = load_w(W_1 + fg * 2 + half)
                        for mc in range(4):
                            fc = half * 4 + mc
                            pbk = pbank()
                            for k in range(8):
                                mm(ps[:, pbk, :], wt[b_][:, k, mc * 128:(mc + 1) * 128], hn[:, k, :], start=(k == 0), stop=(k == 7),
                                   r=[("wt", b_), ("hn", k)], w=[PS(pbk)])
                            ri = fc % 2
                            act(r32[ri][:], ps[:, pbk, :], AF.Relu, r=[PS(pbk)], w=[("r32", ri)])
                            tt("pool", uT[:, fc, :], r32[ri][:], r32[ri][:], ALU.mult, r=[("r32", ri)], w=[ures(fg, fc)])

                def w2s(fg):
                    uT = uTs[fg % 2]
                    for oh in range(2):
                        b_ = load_w(W_2 + fg * 2 + oh)
                        for mc in range(4):
                            m = oh * 4 + mc
                            pbk = pbank()
                            for fc in range(8):
                                mm(ps[:, pbk, :], wt[b_][:, fc, mc * 128:(mc + 1) * 128], uT[:, fc, :], start=(fc == 0), stop=(fc == 7),
                                   r=[("wt", b_), ures(fg, fc)], w=[PS(pbk)])
                            tt("dve", H[:, m, :], ps[:, pbk, :], H[:, m, :], ALU.add, r=[PS(pbk), ("hT", bt, m)], w=[("hT", bt, m)])

                for f_, g_ in ((w1s, 0), (w1s, 1), (w2s, 0), (w1s, 2), (w2s, 1), (w1s, 3), (w2s, 2), (w2s, 3)):
                    f_(g_)
                dma("pool", yT[t], H[:], ("yst", bt), r=[("hT", bt, k) for k in range(8)], w=[("yout", t)])
            P.enabled = True
            P.add("sp", lambda e_: None, [("yout", t) for t in range(NTO)], ["fin"])
            P.emit()
    return nc


def _wtile(w):
    return np.ascontiguousarray(w.reshape(8, 128, 512).transpose(1, 0, 2)).reshape(128, 4096)


def _xt_tiles(xs):
    n = xs.shape[0] // 512
    return np.ascontiguousarray(xs.reshape(n, 512, 8, 128).transpose(0, 3, 2, 1))


def prep_inputs(S, inp):
    f = lambda a: np.asarray(a, dtype=np.float32)
    x, mem = f(inp["x"]), f(inp["mem"])
    NB = S // 128
    w_in = f(inp["w_in"])
    w_ff = np.ascontiguousarray(w_in[:, 1536:1544].reshape(8, 128, 8).transpose(1, 0, 2))
    w_fox = np.stack([
        np.concatenate([w_in[:, hp * 128:(hp + 1) * 128], w_in[:, 512 + hp * 128:512 + (hp + 1) * 128],
                        w_in[:, 1024 + hp * 128:1024 + (hp + 1) * 128]], axis=1).reshape(8, 128, 384).transpose(1, 0, 2)
        for hp in range(4)])
    gq, gk, gv, glr, gr = (w_in[:, 1544:1800], w_in[:, 1800:2056], w_in[:, 2056:2568], w_in[:, 2568:2584],
                           w_in[:, 2584:3096])
    w_gla = np.ascontiguousarray(np.concatenate([gq, gk, gv, gr, glr], axis=1).reshape(8, 128, 1552).transpose(1, 0, 2))
    w_gate2a = np.zeros((33, 256), np.float32)
    w_gate2a[0:16] = f(inp["gla_w_gate2"])
    w_gate2a[32] = f(inp["gla_b_gate"])
    tiles = []
    for w in (f(inp["w_out"]), f(inp["xattn_wq"])):
        tiles += [_wtile(w[:, 0:512]), _wtile(w[:, 512:1024])]
    wkv = f(inp["xattn_wkv"])
    tiles += [_wtile(wkv[:, i * 512:(i + 1) * 512]) for i in range(4)]
    wo = f(inp["xattn_wo"])
    tiles += [_wtile(wo[:, 0:512]), _wtile(wo[:, 512:1024])]
    w1 = f(inp["mlp_w1"])
    tiles += [_wtile(w1[:, i * 512:(i + 1) * 512]) for i in range(8)]
    w2 = f(inp["mlp_w2"])
    for fg in range(4):
        for oh in range(2):
            tiles.append(_wtile(w2[fg * 1024:(fg + 1) * 1024, oh * 512:(oh + 1) * 512]))
    w_all = np.stack(tiles)
    gcols = np.zeros((128, 38), np.float32)
    for base, g in ((G_MIX, inp["norm_mix_g"]), (G_X, inp["norm_xattn_g"]), (G_MEM, inp["norm_mem_g"]),
                    (G_MLP, inp["norm_mlp_g"])):
        gcols[:, base:base + 8] = f(g).reshape(8, 128).T
    gcols[:, G_XQ:G_XQ + 2] = f(inp["xattn_q_norm_g"]).reshape(2, 128).T
    gcols[:, G_XK:G_XK + 2] = f(inp["xattn_k_norm_g"]).reshape(2, 128).T
    gcols[:, G_FQ] = np.tile(f(inp["fox_q_norm_g"]), 2)
    gcols[:, G_FK] = np.tile(f(inp["fox_k_norm_g"]), 2)
    bf_bc = np.ascontiguousarray(np.broadcast_to(np.tile(f(inp["fox_b_f"]), NB)[None, :], (128, NB * 8)))
    gout_bc = np.ascontiguousarray(np.broadcast_to(f(inp["gla_out_norm_g"])[None, :], (128, 512)))
    idx = np.arange(128)
    cst = np.zeros((128, 5, 128), np.float32)
    cst[:, 4, :] = (idx[:, None] // 64 == idx[None, :] // 64)
    cst[:, 0, :] = np.eye(128)
    cst[:, 1, :] = (idx[:, None] <= idx[None, :])
    cst[:, 2, :] = (idx[:, None] > idx[None, :])
    cst[:, 3, :] = 1.0
    causal = np.where(idx[:, None] <= idx[None, :], 0.0, NEG).astype(np.float32)
    gla_mask = np.tile((idx[:, None] <= idx[None, :]).astype(np.float32), (1, 4))
    common = dict(w_ff=w_ff, w_fox=w_fox, w_gla=w_gla, w_gate2a=w_gate2a, w_all=w_all, gcols=gcols, bf_bc=bf_bc,
                  gout_bc=gout_bc, cst=cst, gla_mask=gla_mask)
    maps = []
    for c in range(8):
        b, par = c // 2, c % 2
        xb = x[b]
        xo = xb.reshape(NB // 2, 2, 128, D)[:, par].reshape(-1, D)
        if par == 0:
            mE, mO = causal, np.full((128, 128), NEG, np.float32)
        else:
            mE, mO = np.zeros((128, 128), np.float32), causal
        m = dict(common)
        m["xT_all"] = _xt_tiles(xb)
        m["xT_own"] = _xt_tiles(xo)
        m["memT"] = np.ascontiguousarray(mem[b].reshape(256, 8, 128).transpose(2, 1, 0))
        m["mask_fox"] = np.ascontiguousarray(np.concatenate([mE, mE, mO, mO], axis=1))
        m["parf"] = np.ascontiguousarray(np.broadcast_to(np.array([par, 1 - par], np.float32)[None, :], (128, 2)))
        maps.append(m)
    return maps


def assemble(S, results, B=4):
    NB = S // 128
    out = np.zeros((B, S, D), np.float32)
    for c in range(2 * B):
        b, par = c // 2, c % 2
        yT = results[c]["yT"]
        yo = yT.transpose(0, 3, 2, 1).reshape(-1, D)
        out[b].reshape(NB // 2, 2, 128, D)[:, par] = yo.reshape(NB // 2, 128, D)
    return out


def kernel(**inputs):
    S = inputs["x"].shape[1]
    nc = build(S)
    maps = prep_inputs(S, inputs)
    res = run_bass_kernel_spmd(nc, maps, core_ids=list(range(8)))
    return assemble(S, res.results)
```
